# Optimizing a Trainium2 kernel written in Bass

```python
import math
import jax, jax.numpy as jnp
from jax import lax
import numpy as np

D_MODEL = 1024
BATCH = 1
SEQ = 16384
DEPTH = 1

D_MIX = D_MODEL
D_MLSTM = D_MIX // 2
N_MLSTM_HEADS = 4
MLSTM_HEAD_DIM = D_MLSTM // N_MLSTM_HEADS
CHUNK = 64
CONV_WIDTH = 4
D_SSM = D_MIX - D_MLSTM
SSM_GROUP = 16
N_SSM_GROUPS = D_SSM // SSM_GROUP
SSM_STATE = 64
D_FF = 4 * D_MODEL
EPS = 1e-6
DT_MIN = 1e-3
DT_MAX = 1e-1
IN_COLS = 4 * D_MLSTM + 2 * N_MLSTM_HEADS + D_SSM

kernel_name = 'hymba_mlstm_s5_hybrid_block'


def rmsnorm(x, g):
    xf = x.astype(jnp.float32)
    r = lax.rsqrt(jnp.mean(xf * xf, axis=-1, keepdims=True) + EPS)
    return (xf * r * g.astype(jnp.float32)).astype(x.dtype)


def causal_depthwise_conv(x, w, b):
    S = x.shape[1]
    xp = jnp.pad(x, ((0, 0), (CONV_WIDTH - 1, 0), (0, 0)))
    out = b
    for j in range(CONV_WIDTH):
        out = out + xp[:, j:j + S, :] * w[j]
    return out


def headwise_layernorm(h, w):
    mu = jnp.mean(h, axis=-1, keepdims=True)
    var = jnp.mean(jnp.square(h - mu), axis=-1, keepdims=True)
    hn = (h - mu) * lax.rsqrt(var + EPS)
    Bsz, S, H, D = h.shape
    return hn.reshape(Bsz, S, H * D) * w.astype(jnp.float32)


def mlstm_chunkwise(q, k, v, i_pre, f_pre):
    Bsz, S, H, D = q.shape
    NC, L = S // CHUNK, CHUNK

    def to_chunks(t):
        return t.reshape(Bsz, NC, L, H, D).transpose(0, 3, 1, 2, 4)

    q = to_chunks(q) * (D ** -0.5)
    k = to_chunks(k)
    v = to_chunks(v)
    log_f = jax.nn.log_sigmoid(f_pre).reshape(Bsz, NC, L, H).transpose(0, 3, 1, 2)
    log_i = i_pre.reshape(Bsz, NC, L, H).transpose(0, 3, 1, 2)
    b = jnp.cumsum(log_f, axis=-1)
    b_last = b[..., -1]

    causal = jnp.tril(jnp.ones((L, L), dtype=bool))
    log_d = jnp.where(causal, b[..., :, None] - b[..., None, :] + log_i[..., None, :], -jnp.inf)

    g = b_last[..., None] - b + log_i
    a = jnp.max(g, axis=-1)
    w_g = jnp.exp(g - a[..., None])
    C_loc = jnp.einsum('bhcl,bhcld,bhcle->bhcde', w_g, v, k)
    n_loc = jnp.einsum('bhcl,bhcle->bhce', w_g, k)

    def step(carry, inp):
        C, n, m = carry
        bl, a_c, Cl, nl = inp
        m_new = jnp.maximum(bl + m, a_c)
        s_old = jnp.exp(bl + m - m_new)
        s_new = jnp.exp(a_c - m_new)
        C_new = s_old[..., None, None] * C + s_new[..., None, None] * Cl
        n_new = s_old[..., None] * n + s_new[..., None] * nl
        return (C_new, n_new, m_new), (C, n, m)

    init = (jnp.zeros((Bsz, H, D, D), jnp.float32),
            jnp.zeros((Bsz, H, D), jnp.float32),
            jnp.zeros((Bsz, H), jnp.float32))
    xs = (jnp.moveaxis(b_last, 2, 0), jnp.moveaxis(a, 2, 0),
          jnp.moveaxis(C_loc, 2, 0), jnp.moveaxis(n_loc, 2, 0))
    _, (C_prev, n_prev, m_prev) = lax.scan(step, init, xs)
    C_prev = jnp.moveaxis(C_prev, 0, 2)
    n_prev = jnp.moveaxis(n_prev, 0, 2)
    m_prev = jnp.moveaxis(m_prev, 0, 2)

    inter_log = b + m_prev[..., None]
    m_t = jnp.maximum(inter_log, jnp.max(log_d, axis=-1))
    inter_scale = jnp.exp(inter_log - m_t)
    d_mat = jnp.exp(log_d - m_t[..., None])
    s_qk = jnp.einsum('bhcld,bhcsd->bhcls', q, k) * d_mat
    num = (inter_scale[..., None] * jnp.einsum('bhcde,bhcle->bhcld', C_prev, q)
           + jnp.einsum('bhcls,bhcsd->bhcld', s_qk, v))
    den = inter_scale * jnp.einsum('bhce,bhcle->bhcl', n_prev, q) + jnp.sum(s_qk, axis=-1)
    h = num / jnp.maximum(jnp.abs(den), jnp.exp(-m_t))[..., None]
    return h.transpose(0, 2, 3, 1, 4).reshape(Bsz, S, H, D)


def s5_groups(u, lam_re, lam_im, log_dt, b_re, b_im, c_re, c_im, d):
    f32 = jnp.float32
    lam = lax.complex(lam_re.astype(f32), lam_im.astype(f32))
    dt = jnp.exp(log_dt.astype(f32))[:, None]
    lam_bar = jnp.exp(lam * dt)
    Bc = lax.complex(b_re.astype(f32), b_im.astype(f32))
    B_bar = ((lam_bar - 1.0) / lam)[..., None] * Bc
    Cc = lax.complex(c_re.astype(f32), c_im.astype(f32))
    Bu = jnp.einsum('gpc,bsgc->bsgp', B_bar, u.astype(jnp.complex64))
    A_elems = jnp.broadcast_to(lam_bar, Bu.shape)

    def combine(left, right):
        a_l, x_l = left
        a_r, x_r = right
        return a_r * a_l, a_r * x_l + x_r

    _, states = lax.associative_scan(combine, (A_elems, Bu), axis=1)
    y = jnp.einsum('gcp,bsgp->bsgc', Cc, states).real
    return y + d.astype(f32) * u


def hybrid_mixer(h, w_in, conv_w, conv_b, i_bias, f_bias, mlstm_norm_w,
                 lam_re, lam_im, log_dt, b_re, b_im, c_re, c_im, ssm_d,
                 glu_w, glu_b, w_out):
    Bsz, S, _ = h.shape
    f32 = jnp.float32
    proj = h @ w_in
    qk_pre = proj[..., :2 * D_MLSTM]
    v = proj[..., 2 * D_MLSTM:3 * D_MLSTM]
    o_pre = proj[..., 3 * D_MLSTM:4 * D_MLSTM]
    gates = proj[..., 4 * D_MLSTM:4 * D_MLSTM + 2 * N_MLSTM_HEADS]
    u = proj[..., 4 * D_MLSTM + 2 * N_MLSTM_HEADS:]

    qk = jax.nn.silu(causal_depthwise_conv(qk_pre, conv_w, conv_b)).astype(f32)
    heads = lambda t: t.reshape(Bsz, S, N_MLSTM_HEADS, MLSTM_HEAD_DIM)
    q = heads(qk[..., :D_MLSTM])
    k = heads(qk[..., D_MLSTM:])
    vh = heads(v.astype(f32))
    gates = gates.astype(f32)
    i_pre = gates[..., :N_MLSTM_HEADS] + i_bias.astype(f32)
    f_pre = gates[..., N_MLSTM_HEADS:] + f_bias.astype(f32)
    h_m = mlstm_chunkwise(q, k, vh, i_pre, f_pre)
    h_m = headwise_layernorm(h_m, mlstm_norm_w) * jax.nn.sigmoid(o_pre.astype(f32))

    ug = u.astype(f32).reshape(Bsz, S, N_SSM_GROUPS, SSM_GROUP)
    y = s5_groups(ug, lam_re, lam_im, log_dt, b_re, b_im, c_re, c_im,
                  ssm_d.reshape(N_SSM_GROUPS, SSM_GROUP)).reshape(Bsz, S, D_SSM)
    z = jax.nn.gelu(y)
    y = z * jax.nn.sigmoid(z @ glu_w.astype(f32) + glu_b.astype(f32))

    mixed = jnp.concatenate([h_m, y], axis=-1).astype(h.dtype)
    return mixed @ w_out


def squared_relu_mlp(h, w_ff1, w_ff2):
    a = jax.nn.relu(h @ w_ff1)
    return (a * a) @ w_ff2


def setup_inputs(seed: int = 0) -> dict:
    key = jax.random.key(seed)
    ks = jax.random.split(key, 24)
    f32 = jnp.float32
    nrm = lambda k, shape, s: jax.random.normal(k, shape, f32) * s
    x = jax.random.normal(ks[0], (BATCH, SEQ, D_MODEL), f32)
    mix_norm_w = 1.0 + nrm(ks[1], (DEPTH, D_MODEL), 0.02)
    w_in = nrm(ks[2], (DEPTH, D_MODEL, IN_COLS), D_MODEL ** -0.5)
    conv_w = nrm(ks[3], (DEPTH, CONV_WIDTH, 2 * D_MLSTM), CONV_WIDTH ** -0.5)
    conv_b = nrm(ks[4], (DEPTH, 2 * D_MLSTM), 0.02)
    i_bias = nrm(ks[5], (DEPTH, N_MLSTM_HEADS), 0.1)
    f_bias = jnp.linspace(3.0, 6.0, N_MLSTM_HEADS, dtype=f32)[None, :] + nrm(ks[6], (DEPTH, N_MLSTM_HEADS), 0.01)
    mlstm_norm_w = 1.0 + nrm(ks[7], (DEPTH, D_MLSTM), 0.02)
    ssm_lam_re = -0.5 + nrm(ks[8], (DEPTH, N_SSM_GROUPS, SSM_STATE), 0.01)
    ssm_lam_im = (jnp.pi * jnp.arange(SSM_STATE, dtype=f32))[None, None, :] + nrm(ks[9], (DEPTH, N_SSM_GROUPS, SSM_STATE), 0.01)
    ssm_log_dt = jax.random.uniform(ks[10], (DEPTH, N_SSM_GROUPS), f32, math.log(DT_MIN), math.log(DT_MAX))
    b_scale = (2.0 * SSM_GROUP) ** -0.5
    ssm_b_re = nrm(ks[11], (DEPTH, N_SSM_GROUPS, SSM_STATE, SSM_GROUP), b_scale)
    ssm_b_im = nrm(ks[12], (DEPTH, N_SSM_GROUPS, SSM_STATE, SSM_GROUP), b_scale)
    c_scale = SSM_STATE ** -0.5
    ssm_c_re = nrm(ks[13], (DEPTH, N_SSM_GROUPS, SSM_GROUP, SSM_STATE), c_scale)
    ssm_c_im = nrm(ks[14], (DEPTH, N_SSM_GROUPS, SSM_GROUP, SSM_STATE), c_scale)
    ssm_d = nrm(ks[15], (DEPTH, D_SSM), 1.0)
    glu_w = nrm(ks[16], (DEPTH, D_SSM, D_SSM), D_SSM ** -0.5)
    glu_b = nrm(ks[17], (DEPTH, D_SSM), 0.02)
    w_out = nrm(ks[18], (DEPTH, D_MIX, D_MODEL), D_MIX ** -0.5)
    mlp_norm_w = 1.0 + nrm(ks[19], (DEPTH, D_MODEL), 0.02)
    w_ff1 = nrm(ks[20], (DEPTH, D_MODEL, D_FF), D_MODEL ** -0.5)
    w_ff2 = nrm(ks[21], (DEPTH, D_FF, D_MODEL), D_FF ** -0.5)
    final_norm_w = 1.0 + nrm(ks[22], (D_MODEL,), 0.02)
    return {'x': x, 'mix_norm_w': mix_norm_w, 'w_in': w_in, 'conv_w': conv_w, 'conv_b': conv_b,
            'i_bias': i_bias, 'f_bias': f_bias, 'mlstm_norm_w': mlstm_norm_w,
            'ssm_lam_re': ssm_lam_re, 'ssm_lam_im': ssm_lam_im, 'ssm_log_dt': ssm_log_dt,
            'ssm_b_re': ssm_b_re, 'ssm_b_im': ssm_b_im, 'ssm_c_re': ssm_c_re, 'ssm_c_im': ssm_c_im,
            'ssm_d': ssm_d, 'glu_w': glu_w, 'glu_b': glu_b, 'w_out': w_out,
            'mlp_norm_w': mlp_norm_w, 'w_ff1': w_ff1, 'w_ff2': w_ff2, 'final_norm_w': final_norm_w}


def reference(x, mix_norm_w, w_in, conv_w, conv_b, i_bias, f_bias, mlstm_norm_w,
              ssm_lam_re, ssm_lam_im, ssm_log_dt, ssm_b_re, ssm_b_im, ssm_c_re, ssm_c_im,
              ssm_d, glu_w, glu_b, w_out, mlp_norm_w, w_ff1, w_ff2, final_norm_w):
    for l in range(DEPTH):
        h = rmsnorm(x, mix_norm_w[l])
        x = x + hybrid_mixer(h, w_in[l], conv_w[l], conv_b[l], i_bias[l], f_bias[l], mlstm_norm_w[l],
                             ssm_lam_re[l], ssm_lam_im[l], ssm_log_dt[l], ssm_b_re[l], ssm_b_im[l],
                             ssm_c_re[l], ssm_c_im[l], ssm_d[l], glu_w[l], glu_b[l], w_out[l])
        h = rmsnorm(x, mlp_norm_w[l])
        x = x + squared_relu_mlp(h, w_ff1[l], w_ff2[l])
    return rmsnorm(x, final_norm_w)
```

```python
import math
from contextlib import ExitStack
import numpy as np
import concourse.bass as bass
import concourse.mybir as mybir
from concourse.bass_utils import run_bass_kernel_spmd

F32 = mybir.dt.float32
BF16 = mybir.dt.bfloat16
I32 = mybir.dt.int32
AF = mybir.ActivationFunctionType
ALU = mybir.AluOpType
AX = mybir.AxisListType

NCORES = 8
SEQ = 16384
DM = 1024
TOK = SEQ // NCORES
UT = 512
NSEG = 8
NUNITS = NSEG * TOK // UT
FIRST_OWN = NUNITS - TOK // UT
INC = 2568
EPS = 1e-6
SAME_SYNC = True
SAME_DIST = 10 ** 9
SKIP_PP3 = False
PIPE_PREFIX = True
TWO_PI = 2.0 * math.pi


class Sched:
    def __init__(self, nc, es):
        self.nc = nc
        self.eng = {"pe": nc.tensor, "act": nc.scalar, "dve": nc.vector, "pool": nc.gpsimd, "sp": nc.sync}
        self.csem = {e: es.enter_context(nc.semaphore("c_" + e)) for e in ["pe", "act", "dve", "pool"]}
        self.ccnt = {e: 0 for e in self.csem}
        self.NR = 6
        self.dsem = {q: [es.enter_context(nc.semaphore(f"d_{q}{i}")) for i in range(self.NR)]
                     for q in ["sp", "pool"]}
        self.dcnt = {q: 0 for q in self.dsem}
        self.prog = {e: [] for e in self.eng}
        self.lastw = {}
        self.reads = {}
        self.seen = {e: {} for e in self.eng}
        self.out_tokens = []
        self.alias = {}
        self.tick = 0
        self.ps_touch = [0] * 8

    def canon(self, keys):
        out = []
        for k in keys:
            k2 = self.alias.get(k, self.alias.get(k[0], k) if isinstance(k, tuple) else k)
            if k2 not in out:
                out.append(k2)
        return out

    def barrier(self):
        allt = {("c", e): n for e, n in self.ccnt.items() if n > 0}
        for q, n in self.dcnt.items():
            for i in range(max(0, n - self.NR), n):
                allt[("d", q, i % self.NR)] = 16 * (i // self.NR + 1)
        for E in self.eng:
            waits = []
            for key, val in allt.items():
                if key == ("c", E):
                    continue
                if self.seen[E].get(key, 0) >= val:
                    continue
                self.seen[E][key] = val
                waits.append((self._sem(key), val))
            eng = self.eng[E]

            def run(waits=waits, eng=eng):
                for s_, v in waits:
                    eng.wait_ge(s_, v)
            self.prog[E].append(run)

    def _sem(self, key):
        return self.csem[key[1]] if key[0] == "c" else self.dsem[key[1]][key[2]]

    def _emit(self, E, deps, fn, mykey, myval, inc):
        waits = []
        for key, val in deps.items():
            if key == ("c", E) and (E == "pe" or not SAME_SYNC):
                continue
            if key == ("c", E) and self.ccnt[E] - val > SAME_DIST:
                continue
            if self.seen[E].get(key, 0) >= val:
                continue
            self.seen[E][key] = val
            waits.append((self._sem(key), val))
        mysem = self._sem(mykey)
        eng = self.eng[E]

        def run():
            for s, v in waits:
                eng.wait_ge(s, v)
            fn(eng).then_inc(mysem, inc)
        self.prog[E].append(run)

    def _deps(self, reads, writes):
        deps = {}

        def add(d):
            for k, v in d.items():
                if deps.get(k, 0) < v:
                    deps[k] = v
        for k in list(reads) + list(writes):
            if k in self.lastw:
                add(self.lastw[k])
        for k in writes:
            add(self.reads.get(k, {}))
        return deps

    def _commit(self, reads, writes, tok):
        for k in writes:
            self.lastw[k] = dict(tok)
            self.reads[k] = {}
        for k in reads:
            if k in writes:
                continue
            d = self.reads.setdefault(k, {})
            for kk, v in tok.items():
                if d.get(kk, 0) < v:
                    d[kk] = v

    def op(self, E, fn, reads=(), writes=()):
        reads, writes = self.canon(reads), self.canon(writes)
        for k in reads:
            if isinstance(k, tuple) and k[0] == "ps" and k not in writes:
                writes = list(writes) + [k]
        self.tick += 1
        for k in writes:
            if isinstance(k, tuple) and k[0] == "ps":
                self.ps_touch[k[1]] = self.tick
        deps = self._deps(reads, writes)
        self.ccnt[E] += 1
        key, val = ("c", E), self.ccnt[E]
        self._emit(E, deps, fn, key, val, 1)
        self._commit(reads, writes, {key: val})

    def dma(self, q, out, in_, reads=(), writes=(), is_output=False, r=None, w=None):
        reads = r if r is not None else reads
        writes = w if w is not None else writes
        reads, writes = self.canon(reads), self.canon(writes)
        deps = self._deps(reads, writes)
        i = self.dcnt[q]
        self.dcnt[q] += 1
        slot = i % self.NR
        key, val = ("d", q, slot), 16 * (i // self.NR + 1)
        if i >= self.NR:
            if deps.get(key, 0) < val - 16:
                deps[key] = val - 16
        self._emit(q, deps, lambda e: e.dma_start(out=out, in_=in_), key, val, 16)
        self._commit(reads, writes, {key: val})
        if is_output:
            self.out_tokens.append((key, val))

    def finish(self, block):
        fin = {}
        for key, val in self.out_tokens:
            if fin.get(key, 0) < val:
                fin[key] = val
        sp_waits = [(self._sem(k), v) for k, v in fin.items()]

        def sp_final():
            for s, v in sp_waits:
                self.eng["sp"].wait_ge(s, v)
        self.prog["sp"].append(sp_final)

        @block.sync
        def _(e):
            for f in self.prog["sp"]:
                f()

        @block.tensor
        def _(e):
            for f in self.prog["pe"]:
                f()

        @block.scalar
        def _(e):
            for f in self.prog["act"]:
                f()

        @block.vector
        def _(e):
            for f in self.prog["dve"]:
                f()

        @block.gpsimd
        def _(e):
            for f in self.prog["pool"]:
                f()


def build_nc(nunits=NUNITS, dbg_sel=None, only_units=None, stop=None):
    nc = bass.Bass("TRN2", target_bir_lowering=False)
    es = ExitStack()
    with es:
        def din(name, shape):
            return nc.dram_tensor(name, list(shape), F32, kind="ExternalInput").ap()

        xall = din("xall", [NSEG * TOK, DM])
        d_win = din("w_in", [128, 8, INC])
        d_wout = din("w_out", [128, 8, DM])
        d_gluw = din("glu_w", [128, 4, 512])
        d_wff1 = din("w_ff1", [32, 128, 8, 128])
        d_wff2 = din("w_ff2", [32, 128, DM])
        d_cw = din("cw", [128, 8, 4])
        d_cb = din("cb", [128, 8])
        d_gbias = din("gbias", [128, 8])
        d_nwc = din("mnwc", [128, 4])
        d_g1 = din("g1", [128, 8])
        d_g2 = din("g2", [128, 8])
        d_g3 = din("g3", [128, DM])
        d_glub = din("glub", [128, 4])
        d_segm = din("segm", [128, 8])
        d_lr2 = din("lr2", [128, 32])
        d_li2 = din("li2", [128, 32])
        d_ld2 = din("ld2", [128, 32])
        d_lr1 = din("lr1", [128, 256])
        d_li1 = din("li1", [128, 256])
        d_ld1 = din("ld1", [128, 256])
        d_AB = din("AB", [128, 512])
        d_ABs = din("ABs", [128, 512])
        d_AC = din("AC", [128, 512])
        d_ACs = din("ACs", [128, 512])
        d_bT1r = din("bT1r", [128, 256])
        d_bT1i = din("bT1i", [128, 256])
        d_dpad = din("dpad", [128, 4, 16])
        d_cst = din("cst", [128, 384])
        d_id = din("ident", [128, 128])
        d_tri = din("tri", [128, 128])
        out = nc.dram_tensor("out", [TOK, DM], F32, kind="ExternalOutput").ap()
        if dbg_sel is not None:
            dbg = nc.dram_tensor("dbg", [128, 4096], F32, kind="ExternalOutput").ap()
            dbg2 = nc.dram_tensor("dbg2", [128, 32768], F32, kind="ExternalOutput").ap()
        dmap = {}
        dpos = [0]
        nc._dbg_map = dmap
        wff1_s = nc.dram_tensor("wff1_s", [32, 128, 8, 128], BF16).ap()
        wff2_s = nc.dram_tensor("wff2_s", [32, 128, DM], BF16).ap()
        win_s = nc.dram_tensor("win_s", [128, 8, INC], BF16).ap()

        S = Sched(nc, es)

        def sbx(stack, name, shape, dt=F32):
            return stack.enter_context(nc.sbuf_tensor("s_" + name, list(shape), dt))

        def sb(name, shape, dt=F32):
            return sbx(es, name, shape, dt)

        wvg = sb("wvg", [128, 8, 520], BF16)
        cw = sb("cw", [128, 8, 4]); cb = sb("cb", [128, 8]); gbias = sb("gbias", [128, 8])
        mnwc = sb("mnwc", [128, 4]); g1 = sb("g1", [128, 8]); g2 = sb("g2", [128, 8]); g3 = sb("g3", [128, DM])
        glub = sb("glub", [128, 4]); segm = sb("segm", [128, 8])
        cst = sb("cst", [128, 384])
        ident_b = sb("ident_b", [128, 128], BF16)
        tri = sb("tri", [128, 128]); ones_f = sb("ones_f", [128, 128]); trimask = sb("trimask", [128, 128], BF16)
        ones64 = ones_f[:, 0:64]
        rot_b = sb("rot_b", [128, 128], BF16)
        mhalf = sb("mhalf", [128, 4])
        MV9 = cst[:, 0:9]
        MVE = cst[:, 16:80]
        MVF = cst[:, 80:144]
        MVS = cst[:, 144:152]
        SGNA = cst[:, 152:153]
        SGNB = cst[:, 153:154]
        PAR = cst[:, 154:156]
        MASKG = cst[:, 156:164]
        C512 = cst[:, 164:165]
        C1 = cst[:, 165:166]
        rot = cst[:, 256:384]

        ps = [es.enter_context(nc.psum_tensor(f"ps{i}", [128, 512], F32)) for i in range(8)]
        psc = [0]

        ps_pin = set()

        def nextps(pin=False):
            cand = [b for b in range(8) if b not in ps_pin]
            assert cand, "all PSUM banks pinned"
            i = min(cand, key=lambda b: (S.ps_touch[b], b))
            S.tick += 1
            S.ps_touch[i] = S.tick
            if pin:
                ps_pin.add(i)
            return i

        def unpin(banks):
            for b in banks:
                ps_pin.discard(b)

        def E(eng_, method, *args, r=(), w=(), **kw):
            S.op(eng_, lambda e: getattr(e, method)(*args, **kw), reads=r, writes=w)

        def load(dst, src, key, q="sp"):
            S.dma(q, dst, src, w=[key])

        dtmp = sb("dtmp", [128, 512]) if dbg_sel == "unit" else None

        def dump(name, ap, keys, npart=128):
            if dbg_sel != "unit":
                return
            n = ap.shape[1]
            E("dve", "tensor_copy", out=dtmp[0:npart, 0:n], in_=ap, r=list(keys), w=["dtmp"])
            S.dma("sp", dbg2[0:npart, dpos[0]:dpos[0] + n], dtmp[0:npart, 0:n], reads=["dtmp"], is_output=True)
            dmap[name] = (dpos[0], n, npart)
            dpos[0] += n

        load(wvg[:, :, 0:512], d_win[:, :, 1024:1536], "wvg", q="pool")
        load(wvg[:, :, 512:520], d_win[:, :, 2048:2056], "wvg", q="pool")
        load(ident_b[:], d_id, "ident_b", q="pool")
        load(trimask[:], d_tri, "trimask", q="pool")
        for t_, d_, k_ in [(cw, d_cw, "cw"), (cb, d_cb, "cb"), (gbias, d_gbias, "gbias"), (mnwc, d_nwc, "mnwc"),
                           (g1, d_g1, "g1"), (g2, d_g2, "g2"), (g3, d_g3, "g3"), (glub, d_glub, "glub"),
                           (segm, d_segm, "segm"), (cst, d_cst, "cst"), (tri, d_tri, "tri")]:
            load(t_[:], d_, k_)
        E("dve", "memset", ones_f[:], 1.0, w=["ones_f"])
        E("dve", "tensor_copy", out=rot_b[:], in_=cst[:, 256:384], r=["cst"], w=["rot_b"])
        E("pool", "memset", mhalf[:], -0.5, w=["mhalf"])
        for kc in range(8):
            S.dma("pool", win_s[:, kc, :], d_win[:, kc, :], w=["win_s"])
        for fc in range(32):
            S.dma("pool", wff1_s[fc], d_wff1[fc], w=[("wff1_s", fc)])
            S.dma("pool", wff2_s[fc], d_wff2[fc], w=[("wff2_s", fc)])

        def tt(e_, out_, a, b, op, r, w):
            E(e_, "tensor_tensor", out=out_, in0=a, in1=b, op=op, r=r, w=w)

        def ts(e_, out_, a, s1, s2, op0, op1, r, w):
            E(e_, "tensor_scalar", out=out_, in0=a, scalar1=s1, scalar2=s2, op0=op0, op1=op1, r=r, w=w)

        def stt(e_, out_, a, sc, b, op0, op1, r, w):
            E(e_, "scalar_tensor_tensor", out=out_, in0=a, scalar=sc, in1=b, op0=op0, op1=op1, r=r, w=w)

        def act(out_, in_, func, r, w, bias=None, scale=None):
            kw = {}
            if bias is not None:
                kw["bias"] = bias
            if scale is not None:
                kw["scale"] = scale
            E("act", "activation", out=out_, in_=in_, func=func, **kw, r=r, w=w)

        Er = sb("Er", [128, 32, 64]); Ei = sb("Ei", [128, 32, 64])
        PWB = sb("PWB", [128, 4, 2, 8, 128], BF16)
        Fr_d = nc.dram_tensor("Fr_d", [128, 2048], F32).ap(); Fi_d = nc.dram_tensor("Fi_d", [128, 2048], F32).ap()
        CZ_d = nc.dram_tensor("CZ_d", [128, 4608], BF16).ap(); PWT_d = nc.dram_tensor("PWT_d", [128, 8192], BF16).ap()
        L5r = sb("L5r", [128, 1, 32]); L5i = sb("L5i", [128, 1, 32])
        Scar = sb("Scar", [128, 2, 32])
        Cst_f = sb("Cst_f", [128, 4, 129]); Cst_b = sb("Cst_b", [128, 4, 129], BF16)
        hal = sb("hal", [128, 8, 3])

        with ExitStack() as ps_es:
            def sp(name, shape, dt=F32):
                return sbx(ps_es, name, shape, dt)
            Fr = sp("Fr", [128, 32, 64]); Fi = sp("Fi", [128, 32, 64])
            CZ = sp("CZ", [128, 32, 9, 16], BF16)
            PWT = sp("PWT", [128, 4, 2, 8, 128], BF16)
            PWd = 2048
            t_a = sp("t_a", [128, PWd]); t_k = sp("t_k", [128, PWd]); t_s4 = sp("t_s4", [128, PWd]); t_m = sp("t_m", [128, PWd])
            TK = ["tk"]

            def cpow(theta3, rho3, m3, A, B, out_re, out_im, rk, wk):
                n = A * B

                def v(t, dt=None):
                    ap = t[:, 0:n]
                    if dt is not None:
                        ap = ap.bitcast(dt)
                    return ap.rearrange("p (a b) -> p a b", a=A)
                tt("dve", v(t_a), theta3, m3, ALU.mult, rk + ["cst"] + TK, TK)
                ts("dve", v(t_k), v(t_a), 1.0 / TWO_PI, None, ALU.mult, ALU.bypass, TK, TK)
                E("dve", "tensor_copy", out=v(t_m, I32), in_=v(t_k), r=TK, w=TK)
                E("dve", "tensor_copy", out=v(t_k), in_=v(t_m, I32), r=TK, w=TK)
                stt("dve", v(t_a), v(t_k), -TWO_PI, v(t_a), ALU.mult, ALU.add, TK, TK)
                act(v(t_k), v(t_a), AF.Sin, TK, TK, scale=0.5)
                act(v(t_s4), v(t_a), AF.Sin, TK, TK, scale=0.25)
                tt("dve", v(t_s4), v(t_s4), v(t_s4), ALU.mult, TK, TK)
                ts("dve", v(t_s4), v(t_s4), -2.0, 1.0, ALU.mult, ALU.add, TK, TK)
                stt("dve", v(t_a), v(t_k), 2.0, v(t_s4), ALU.mult, ALU.mult, TK, TK)
                tt("dve", v(t_s4), v(t_k), v(t_k), ALU.mult, TK, TK)
                ts("dve", v(t_s4), v(t_s4), -2.0, 1.0, ALU.mult, ALU.add, TK, TK)
                tt("dve", v(t_m), rho3, m3, ALU.mult, rk + ["cst"] + TK, TK)
                act(v(t_m), v(t_m), AF.Exp, TK, TK)
                tt("dve", out_re, v(t_m), v(t_s4), ALU.mult, TK, wk)
                tt("dve", out_im, v(t_m), v(t_a), ALU.mult, TK, wk)

            def exp_acc(x, shape, key, tmpn):
                y = sp(tmpn + "_y", shape); p = sp(tmpn + "_p", shape)
                T = [tmpn]
                ts("dve", y[:], x, 0.125, None, ALU.mult, ALU.bypass, [key], T)
                E("dve", "memset", p[:], 1.0, r=T, w=T)
                for k in range(12, 0, -1):
                    tt("dve", p[:], p[:], y[:], ALU.mult, T, T)
                    ts("dve", p[:], p[:], 1.0 / k, 1.0, ALU.mult, ALU.add, T, T)
                for _ in range(3):
                    tt("dve", p[:], p[:], p[:], ALU.mult, T, T)
                E("dve", "tensor_copy", out=x, in_=p[:], r=T + [key], w=[key])

            lr2 = sp("lr2", [128, 32]); li2 = sp("li2", [128, 32]); dt2 = sp("dt2", [128, 32])
            rho2 = sp("rho2", [128, 32]); th2 = sp("th2", [128, 32])
            load(lr2[:], d_lr2, "lr2"); load(li2[:], d_li2, "li2"); load(dt2[:], d_ld2, "dt2")
            exp_acc(dt2[:], [128, 32], "dt2", "ex2")
            tt("dve", rho2[:], lr2[:], dt2[:], ALU.mult, ["lr2", "dt2"], ["rho2"])
            tt("dve", th2[:], li2[:], dt2[:], ALU.mult, ["li2", "dt2"], ["th2"])
            Ld_re = sp("Ld_re", [128, 9, 32]); Ld_im = sp("Ld_im", [128, 9, 32])
            cpow(th2[:].unsqueeze(1).to_broadcast([128, 9, 32]), rho2[:].unsqueeze(1).to_broadcast([128, 9, 32]),
                 MV9.unsqueeze(2).to_broadcast([128, 9, 32]), 9, 32, Ld_re[:], Ld_im[:], ["th2", "rho2"], ["Ld"])

            def kappa(lr, li, lbr, lbi, kr, ki, shape, rk, wk, tmpn):
                a = sp(tmpn + "_a", shape); b = sp(tmpn + "_b", shape); c = sp(tmpn + "_c", shape)
                T = [tmpn]
                ts("dve", a[:], lbr, -1.0, None, ALU.add, ALU.bypass, rk, T)
                tt("dve", b[:], lr, lr, ALU.mult, rk + T, T)
                tt("dve", c[:], li, li, ALU.mult, rk + T, T)
                tt("dve", b[:], b[:], c[:], ALU.add, T, T)
                E("dve", "reciprocal", out=b[:], in_=b[:], r=T, w=T)
                tt("dve", kr, a[:], lr, ALU.mult, rk + T, wk)
                tt("dve", c[:], lbi, li, ALU.mult, rk + T, T)
                tt("dve", kr, kr, c[:], ALU.add, wk + T, wk)
                tt("dve", kr, kr, b[:], ALU.mult, wk + T, wk)
                tt("dve", ki, lbi, lr, ALU.mult, rk + T, wk)
                tt("dve", c[:], a[:], li, ALU.mult, rk + T, T)
                tt("dve", ki, ki, c[:], ALU.subtract, wk + T, wk)
                tt("dve", ki, ki, b[:], ALU.mult, wk + T, wk)

            k2r = sp("k2r", [128, 32]); k2i = sp("k2i", [128, 32])
            kappa(lr2[:], li2[:], Ld_re[:, 1, :], Ld_im[:, 1, :], k2r[:], k2i[:], [128, 32], ["lr2", "li2", "Ld"], ["k2"], "kp2")
            AB = sp("AB", [128, 32, 16]); ABs = sp("ABs", [128, 32, 16]); AC = sp("AC", [128, 32, 16]); ACs = sp("ACs", [128, 32, 16])
            load(AB[:].rearrange("p g c -> p (g c)"), d_AB, "AB"); load(ABs[:].rearrange("p g c -> p (g c)"), d_ABs, "ABs")
            load(AC[:].rearrange("p g c -> p (g c)"), d_AC, "AC"); load(ACs[:].rearrange("p g c -> p (g c)"), d_ACs, "ACs")
            k2is = sp("k2is", [128, 32])
            ts("dve", k2is[:], k2i[:], SGNA, None, ALU.mult, ALU.bypass, ["k2", "cst"], ["k2is"])
            tmpB = sp("tmpB", [128, 32, 16]); tmpB2 = sp("tmpB2", [128, 32, 16])
            BbS = sp("BbS", [128, 32, 16], BF16)
            tt("dve", tmpB[:], AB[:], k2r[:].unsqueeze(2).to_broadcast([128, 32, 16]), ALU.mult, ["AB", "k2"], ["tmpB"])
            tt("dve", tmpB2[:], ABs[:], k2is[:].unsqueeze(2).to_broadcast([128, 32, 16]), ALU.mult, ["ABs", "k2is"], ["tmpB2"])
            tt("dve", BbS[:], tmpB[:], tmpB2[:], ALU.add, ["tmpB", "tmpB2"], ["BbS"])
            P1 = sp("P1", [128, 9, 32]); P2 = sp("P2", [128, 9, 32])
            ts("dve", P1[:], Ld_re[:], SGNB, None, ALU.mult, ALU.bypass, ["Ld", "cst"], ["P1"])
            ts("dve", P2[:], Ld_im[:], -1.0, None, ALU.mult, ALU.bypass, ["Ld"], ["P2"])
            for d in range(9):
                tt("dve", tmpB[:], AC[:], P1[:, d, :].unsqueeze(2).to_broadcast([128, 32, 16]), ALU.mult, ["AC", "P1", "tmpB"], ["tmpB"])
                tt("dve", tmpB2[:], ACs[:], P2[:, d, :].unsqueeze(2).to_broadcast([128, 32, 16]), ALU.mult, ["ACs", "P2", "tmpB2"], ["tmpB2"])
                tt("dve", CZ[:, :, d, :], tmpB[:], tmpB2[:], ALU.add, ["tmpB", "tmpB2"], ["CZ"])
            thb = th2[:].unsqueeze(2).to_broadcast([128, 32, 64]); rhb = rho2[:].unsqueeze(2).to_broadcast([128, 32, 64])
            cpow(thb, rhb, MVE.unsqueeze(1).to_broadcast([128, 32, 64]), 32, 64, Er[:], Ei[:], ["th2", "rho2"], ["Etab"])
            m2 = t_a[:, 0:2048].rearrange("p (a b) -> p a b", a=32)
            m3 = t_k[:, 0:2048].rearrange("p (a b) -> p a b", a=32)
            tt("dve", m2, Er[:], Er[:], ALU.mult, ["Etab"] + TK, TK)
            tt("dve", m3, Ei[:], Ei[:], ALU.mult, ["Etab"] + TK, TK)
            tt("dve", m2, m2, m3, ALU.add, TK, TK)
            E("dve", "reciprocal", out=m2, in_=m2, r=TK, w=TK)
            tt("dve", Fr[:], Er[:], m2, ALU.mult, ["Etab"] + TK, ["Ftab"])
            stt("dve", Fi[:], Ei[:], -1.0, m2, ALU.mult, ALU.mult, ["Etab"] + TK, ["Ftab"])
            cpow(th2[:].unsqueeze(1), rho2[:].unsqueeze(1), C512.unsqueeze(2).to_broadcast([128, 1, 32]), 1, 32,
                 L5r[:], L5i[:], ["th2", "rho2"], ["L5"])
            lr1 = sp("lr1", [128, 256]); li1 = sp("li1", [128, 256]); dt1 = sp("dt1", [128, 256])
            rho1 = sp("rho1", [128, 256]); th1 = sp("th1", [128, 256])
            load(lr1[:], d_lr1, "lr1"); load(li1[:], d_li1, "li1"); load(dt1[:], d_ld1, "dt1")
            exp_acc(dt1[:], [128, 256], "dt1", "ex1")
            tt("dve", rho1[:], lr1[:], dt1[:], ALU.mult, ["lr1", "dt1"], ["rho1"])
            tt("dve", th1[:], li1[:], dt1[:], ALU.mult, ["li1", "dt1"], ["th1"])
            lb1r = sp("lb1r", [128, 1, 256]); lb1i = sp("lb1i", [128, 1, 256])
            cpow(th1[:].unsqueeze(1), rho1[:].unsqueeze(1), C1.unsqueeze(2).to_broadcast([128, 1, 256]), 1, 256,
                 lb1r[:], lb1i[:], ["th1", "rho1"], ["lb1"])
            k1r = sp("k1r", [128, 256]); k1i = sp("k1i", [128, 256])
            kappa(lr1[:], li1[:], lb1r[:, 0, :], lb1i[:, 0, :], k1r[:], k1i[:], [128, 256], ["lr1", "li1", "lb1"], ["k1"], "kp1")
            Fs_r = sp("Fs_r", [128, 8, 256]); Fs_i = sp("Fs_i", [128, 8, 256])
            cpow(th1[:].unsqueeze(1).to_broadcast([128, 8, 256]), rho1[:].unsqueeze(1).to_broadcast([128, 8, 256]),
                 MVS.unsqueeze(2).to_broadcast([128, 8, 256]), 8, 256, Fs_r[:], Fs_i[:], ["th1", "rho1"], ["Fs"])
            bT1r = sp("bT1r", [128, 256]); bT1i = sp("bT1i", [128, 256])
            load(bT1r[:], d_bT1r, "bT1r"); load(bT1i[:], d_bT1i, "bT1i")
            kbr = sp("kbr", [128, 256]); kbi = sp("kbi", [128, 256]); tq = sp("tq", [128, 256])
            tt("dve", kbr[:], k1r[:], bT1r[:], ALU.mult, ["k1", "bT1r"], ["kbr"])
            tt("dve", tq[:], k1i[:], bT1i[:], ALU.mult, ["k1", "bT1i"], ["tq"])
            tt("dve", kbr[:], kbr[:], tq[:], ALU.subtract, ["kbr", "tq"], ["kbr"])
            tt("dve", kbi[:], k1r[:], bT1i[:], ALU.mult, ["k1", "bT1i"], ["kbi"])
            tt("dve", tq[:], k1i[:], bT1r[:], ALU.mult, ["k1", "bT1r", "kbr"], ["tq"])
            tt("dve", kbi[:], kbi[:], tq[:], ALU.add, ["kbi", "tq"], ["kbi"])
            Bn_r = t_a[:, 0:2048].rearrange("p (a b) -> p a b", a=8)
            Bn_i = t_k[:, 0:2048].rearrange("p (a b) -> p a b", a=8)
            t8 = t_s4[:, 0:2048].rearrange("p (a b) -> p a b", a=8)
            kbrb = kbr[:].unsqueeze(1).to_broadcast([128, 8, 256]); kbib = kbi[:].unsqueeze(1).to_broadcast([128, 8, 256])
            tt("dve", Bn_r, Fs_r[:], kbrb, ALU.mult, ["Fs", "kbr"] + TK, TK)
            tt("dve", t8, Fs_i[:], kbib, ALU.mult, ["Fs", "kbi"] + TK, TK)
            tt("dve", Bn_r, Bn_r, t8, ALU.subtract, TK, TK)
            tt("dve", Bn_i, Fs_r[:], kbib, ALU.mult, ["Fs", "kbi"] + TK, TK)
            tt("dve", t8, Fs_i[:], kbrb, ALU.mult, ["Fs", "kbr"] + TK, TK)
            tt("dve", Bn_i, Bn_i, t8, ALU.add, TK, TK)
            for j in range(4):
                for e_ in range(2):
                    pe_ = PAR[:, e_:e_ + 1]
                    ts("dve", PWB[:, j, e_, :, 0:64], Bn_r[:, :, j * 64:(j + 1) * 64], pe_, None, ALU.mult, ALU.bypass, TK + ["cst"], ["PWB"])
                    ts("dve", PWB[:, j, e_, :, 64:128], Bn_i[:, :, j * 64:(j + 1) * 64], pe_, None, ALU.mult, ALU.bypass, TK + ["cst"], ["PWB"])
            Kt = sp("Kt", [128, 4, 2, 128])
            E("dve", "memset", Kt[:], 0.0, w=["Kt"])
            dpad = sp("dpad", [128, 4, 16])
            load(dpad[:], d_dpad, "dpad")
            for g in range(32):
                j, g8 = g // 8, g % 8
                e_ = g8 % 2
                pi = nextps()
                E("pe", "matmul", ps[pi][:, 0:128], BbS[:, 8 * j:8 * j + 8, :].rearrange("p g c -> p (g c)"),
                                                             CZ[:, g, 0:8, :].rearrange("p d c -> p (d c)"), start=True, stop=True, r=["BbS", "CZ"], w=[("ps", pi)])
                stt("dve", Kt[:, j, e_, :], ps[pi][:, 0:128], MASKG[:, g8:g8 + 1], Kt[:, j, e_, :], ALU.mult, ALU.add,
                    [("ps", pi), "cst", "Kt"], ["Kt"])
            for j in range(4):
                for e_ in range(2):
                    stt("dve", Kt[:, j, e_, 0:16], dpad[:, j, :], PAR[:, e_:e_ + 1], Kt[:, j, e_, 0:16], ALU.mult, ALU.add,
                        ["dpad", "cst", "Kt"], ["Kt"])
            E("pool", "memset", PWT[:], 0.0, w=["PWT"])
            for j in range(4):
                for sg in range(8):
                    L = (8 - sg) * 16
                    E("dve", "tensor_copy", out=PWT[:, j, :, sg, sg * 16:128], in_=Kt[:, j, :, 0:L], r=["Kt"], w=["PWT"])
            if dbg_sel == "prep":
                S.dma("sp", dbg[:, 0:288], Ld_re[:].rearrange("p a b -> p (a b)"), r=["Ld"], is_output=True)
                S.dma("sp", dbg[:, 288:576], Ld_im[:].rearrange("p a b -> p (a b)"), r=["Ld"], is_output=True)
                S.dma("sp", dbg[:, 576:608], k2r[:], reads=["k2"], is_output=True)
                S.dma("sp", dbg[:, 608:640], k2i[:], reads=["k2"], is_output=True)
                S.dma("sp", dbg[:, 1024:2048], Kt[:].rearrange("p a b c -> p (a b c)"), r=["Kt"], is_output=True)
                S.dma("sp", dbg[:, 2048:4096], Er[:].rearrange("p a b -> p (a b)"), r=["Etab"], is_output=True)
            S.dma("sp", Fr_d, Fr[:].rearrange("p g n -> p (g n)"), reads=["Ftab"], writes=["Fr_d"])
            S.dma("sp", Fi_d, Fi[:].rearrange("p g n -> p (g n)"), reads=["Ftab"], writes=["Fi_d"])
            S.dma("sp", CZ_d, CZ[:].rearrange("p g d c -> p (g d c)"), reads=["CZ"], writes=["CZ_d"])
            S.dma("sp", PWT_d, PWT[:].rearrange("p j e s m -> p (j e s m)"), reads=["PWT"], writes=["PWT_d"])
            S.barrier()

        E("dve", "memset", Cst_f[:], 0.0, w=["Cst_f"])
        E("dve", "memset", Cst_b[:], 0.0, w=["Cst_b"])
        E("dve", "memset", Scar[:], 0.0, w=["Scar"])
        E("dve", "memset", hal[:], 0.0, w=["hal"])

        unit_list = list(only_units) if only_units is not None else list(range(NUNITS - nunits, NUNITS))
        pre_units = [u for u in unit_list if u < FIRST_OWN]
        own_units = [u for u in unit_list if u >= FIRST_OWN]
        if pre_units and PIPE_PREFIX:
            with ExitStack() as px:
                def sq(name, shape, dt=F32):
                    return sbx(px, name, shape, dt)
                NXR = 8
                xr = [sq(f"xr{i}", [128, DM]) for i in range(NXR)]
                xnr = [sq(f"xnr{i}", [128, DM], BF16) for i in range(4)]
                bst_p = sq("bst_p", [128, 2, 6]); bag_p = sq("bag_p", [128, 2])
                rs_p = [sq(f"rs_p{b}", [128, 8]) for b in range(2)]
                hTp = [sq(f"hTp{b}", [128, 8, UT], BF16) for b in range(2)]
                preb = [sq(f"preb{b}", [128, UT + 3], BF16) for b in range(2)]
                kTp = [sq(f"kTp{b}", [128, 4, UT], BF16) for b in range(2)]
                uTp = [sq(f"uTp{b}", [128, 4, UT], BF16) for b in range(2)]
                vtp = [sq(f"vtp{b}", [128, 4, 4, 129], BF16) for b in range(2)]
                gatp = [sq(f"gatp{b}", [128, 4, 8]) for b in range(2)]
                splp = [sq(f"splp{b}", [128, 4, 4]) for b in range(2)]
                gap = [sq(f"gap{b}", [128, 4, 4]) for b in range(2)]
                gkLp = [sq(f"gkLp{b}", [128, 4, 4]) for b in range(2)]
                sufp = [sq(f"sufp{b}", [128, 4, 4]) for b in range(2)]
                decp = [sq(f"decp{b}", [128, 4]) for b in range(2)]
                khp = [sq(f"khp{b}", [128, 4, 4, 128], BF16) for b in range(2)]
                Wsp = [sq(f"Wsp{i}", [128, 256], BF16) for i in range(4)]
                ZtP = [sq(f"ZtP{b}", [128, 32, 64]) for b in range(2)]
                ztp = [sq(f"ztp{i}", [128, 256]) for i in range(2)]
                sumz = sq("sumz", [128, 32]); send = sq("send", [128, 32]); cr2 = sq("cr2", [128, 2, 32])
                wku = sq("wku", [128, 8, 1024], BF16)
                Dg = sq("Dg", [128, 4, 4, 128], BF16)
                halb = sq("halb", [128, 4, 3], BF16)
                for kc in range(8):
                    S.dma("sp", wku[:, kc, 0:512], win_s[:, kc, 512:1024], reads=["win_s"], writes=["wku"])
                    S.dma("sp", wku[:, kc, 512:1024], win_s[:, kc, 2056:2568], reads=["win_s"], writes=["wku"])
                E("dve", "memset", halb[:], 0.0, w=["halb"])
                for b in range(2):
                    E("pool", "memset", vtp[b][:], 1.0, w=[("vtp", b)])
                    E("pool", "memset", sufp[b][:], 0.0, w=[("sufp", b)])
                for c in range(4):
                    for jj in range(4):
                        ts("dve", Dg[:, c, jj, :], ident_b[:], cw[:, 4 + c, jj:jj + 1], None, ALU.mult, ALU.bypass,
                           ["ident_b", "cw"], ["Dg"])
                Er5 = Er[:].rearrange("p (j q e) n -> p j q e n", j=4, q=4)
                Ei5 = Ei[:].rearrange("p (j q e) n -> p j q e n", j=4, q=4)

                def stageA1(u):
                    b = u % 2
                    tok0 = u * UT
                    for t in range(4):
                        xi = (4 * u + t) % NXR
                        X = xr[xi]
                        S.dma("sp", X[:], xall[tok0 + t * 128: tok0 + (t + 1) * 128, :], writes=[("xr", xi)])
                        E("dve", "bn_stats", out=bst_p[:, 0, :], in_=X[:, 0:512], r=[("xr", xi)], w=["bst_p"])
                        E("dve", "bn_stats", out=bst_p[:, 1, :], in_=X[:, 512:1024], r=[("xr", xi)], w=["bst_p"])
                        E("dve", "bn_aggr", out=bag_p[:], in_=bst_p[:].rearrange("p a b -> p (a b)"), r=["bst_p"], w=["bag_p"])
                        stt("dve", rs_p[b][:, t:t + 1], bag_p[:, 0:1], bag_p[:, 0:1], bag_p[:, 1:2], ALU.mult, ALU.add,
                            ["bag_p"], [("rs_p", b)])
                    ts("dve", rs_p[b][:, 4:8], rs_p[b][:, 0:4], EPS, None, ALU.add, ALU.bypass, [("rs_p", b)], [("rs_p", b)])

                def stageA1b(u):
                    b = u % 2
                    act(rs_p[b][:, 4:8], rs_p[b][:, 4:8], AF.Sqrt, [("rs_p", b)], [("rs_p", b)])
                    E("dve", "reciprocal", out=rs_p[b][:, 4:8], in_=rs_p[b][:, 4:8], r=[("rs_p", b)], w=[("rs_p", b)])

                def stageA2(u):
                    b = u % 2
                    for t in range(4):
                        xi = (4 * u + t) % NXR
                        act(xnr[t][:], xr[xi][:], AF.Copy, [("xr", xi), ("rs_p", b)], [("xnr", t)], scale=rs_p[b][:, 4 + t:5 + t])
                    for tp_ in range(2):
                        pbs = [nextps(), nextps()]
                        for t in (2 * tp_, 2 * tp_ + 1):
                            for c in range(8):
                                pi = pbs[c // 4]
                                pb = ps[pi][:].bitcast(BF16)
                                col = ((c % 4) * 2 + (t % 2)) * 128
                                E("pe", "transpose", pb[:, col:col + 128], xnr[t][:, c * 128:(c + 1) * 128], ident_b[:],
                                  r=[("xnr", t), "ident_b"], w=[("ps", pi)])
                        t = 2 * tp_ + 1
                        for half in range(2):
                            pi = pbs[half]
                            pb = ps[pi][:].bitcast(BF16)
                            for cc in range(4):
                                c = half * 4 + cc
                                dst = hTp[b][:, c, (t - 1) * 128:(t + 1) * 128]
                                src = pb[:, cc * 256:(cc + 1) * 256]
                                if half == 0:
                                    act(dst, src, AF.Copy, [("ps", pi), "g1"], [("hTp", b, c)], scale=g1[:, c:c + 1])
                                else:
                                    ts("dve", dst, src, g1[:, c:c + 1], None, ALU.mult, ALU.bypass, [("ps", pi), "g1"], [("hTp", b, c)])

                def stageB(u):
                    b = u % 2
                    seg = u // 4
                    hT_ = hTp[b]
                    def kproj(c):
                        pi = nextps()
                        for kc in range(8):
                            E("pe", "matmul", ps[pi][:], wku[:, kc, c * 128:(c + 1) * 128], hT_[:, kc, :], start=(kc == 0), stop=(kc == 7),
                              r=["wku", ("hTp", b, kc)], w=[("ps", pi)])
                        pr = preb[c % 2]
                        kp = ("preb", c % 2)
                        E("pool", "tensor_copy", out=pr[:, 0:3], in_=halb[:, c, :], r=["halb"], w=[kp])
                        act(pr[:, 3:UT + 3], ps[pi][:], AF.Copy, [("ps", pi)], [kp])
                        E("pool", "tensor_copy", out=halb[:, c, :], in_=pr[:, UT:UT + 3], r=[kp], w=["halb"])

                    def kconv(c):
                        pr = preb[c % 2]
                        kp = ("preb", c % 2)
                        pj = nextps()
                        for jj in range(4):
                            E("pe", "matmul", ps[pj][:], Dg[:, c, jj, :], pr[:, jj:jj + UT], start=(jj == 0), stop=(jj == 3),
                              r=["Dg", kp], w=[("ps", pj)])
                        act(kTp[b][:, c, :], ps[pj][:], AF.Silu, [("ps", pj), "cb"], [("kTp", b)], bias=cb[:, 4 + c:5 + c])
                    for c in range(4):
                        kproj(c)
                        if c > 0:
                            kconv(c - 1)
                    kconv(3)
                    for c in range(4):
                        pi = nextps()
                        for kc in range(8):
                            E("pe", "matmul", ps[pi][:], wku[:, kc, 512 + c * 128:512 + (c + 1) * 128], hT_[:, kc, :],
                              start=(kc == 0), stop=(kc == 7), r=["wku", ("hTp", b, kc)], w=[("ps", pi)])
                        act(uTp[b][:, c, :], ps[pi][:], AF.Copy, [("ps", pi)], [("uTp", b)])
                    if u == FIRST_OWN - 1:
                        for c in range(4):
                            bi = wsl[0] % 2
                            wsl[0] += 1
                            wb = w1b_early[bi]
                            S.dma("sp", wb[:], win_s[:, :, c * 128:(c + 1) * 128], reads=["win_s"], writes=[("w1be", bi)])
                            pi = nextps()
                            for kc in range(8):
                                E("pe", "matmul", ps[pi][:], wb[:, kc, :], hT_[:, kc, :], start=(kc == 0), stop=(kc == 7),
                                  r=[("w1be", bi), ("hTp", b, kc)], w=[("ps", pi)])
                            E("dve", "tensor_copy", out=hal[:, c, :], in_=ps[pi][:, UT - 3:UT], r=[("ps", pi)], w=["hal"])
                    pg = nextps()
                    for t in range(4):
                        for kc in range(8):
                            E("pe", "matmul", ps[pg][:, t * 8:(t + 1) * 8], hT_[:, kc, t * 128:(t + 1) * 128], wvg[:, kc, 512:520],
                              start=(kc == 0), stop=(kc == 7), r=["wvg", ("hTp", b, kc)], w=[("ps", pg)])
                    G_ = [("gatp", b)]
                    tt("dve", gatp[b][:], ps[pg][:, 0:32].rearrange("p (t g) -> p t g", t=4),
                       gbias[:].unsqueeze(1).to_broadcast([128, 4, 8]), ALU.add, [("ps", pg), "gbias"], G_)
                    act(splp[b][:], gatp[b][:, :, 4:8], AF.Exp, G_, [("splp", b)], scale=-1.0)
                    act(splp[b][:], splp[b][:], AF.Ln, [("splp", b)], [("splp", b)], bias=1.0)
                    for t in range(4):
                        pi = nextps()
                        for kc in range(8):
                            E("pe", "matmul", ps[pi][:], hT_[:, kc, t * 128:(t + 1) * 128], wvg[:, kc, 0:512], start=(kc == 0), stop=(kc == 7),
                              r=["wvg", ("hTp", b, kc)], w=[("ps", pi)])
                        act(vtp[b][:, t, :, 0:128], ps[pi][:].rearrange("p (h d) -> p h d", h=4), AF.Copy, [("ps", pi)], [("vtp", b)])
                    pq = nextps()
                    sp2 = splp[b][:].rearrange("p t h -> p (t h)")
                    E("pe", "matmul", ps[pq][:, 0:16], tri[:], sp2, start=True, stop=True, r=["tri", ("splp", b)], w=[("ps", pq)])
                    E("pe", "matmul", ps[pq][:, 16:32], ones_f[:], sp2, start=True, stop=True, r=["ones_f", ("splp", b)], w=[("ps", pq)])
                    cums = ps[pq][:, 0:16].rearrange("p (t h) -> p t h", t=4)
                    tot = ps[pq][:, 16:32].rearrange("p (t h) -> p t h", t=4)
                    E("dve", "tensor_copy", out=sufp[b][:, 2, :], in_=tot[:, 3, :], r=[("ps", pq)], w=[("sufp", b)])
                    tt("dve", sufp[b][:, 1, :], sufp[b][:, 2, :], tot[:, 2, :], ALU.add, [("ps", pq), ("sufp", b)], [("sufp", b)])
                    tt("dve", sufp[b][:, 0, :], sufp[b][:, 1, :], tot[:, 1, :], ALU.add, [("ps", pq), ("sufp", b)], [("sufp", b)])
                    tt("dve", gap[b][:], gatp[b][:, :, 0:4], cums, ALU.add, G_ + [("ps", pq)], [("gap", b)])
                    tt("dve", gap[b][:], gap[b][:], tot, ALU.subtract, [("gap", b), ("ps", pq)], [("gap", b)])
                    tt("dve", gap[b][:], gap[b][:], sufp[b][:], ALU.subtract, [("gap", b), ("sufp", b)], [("gap", b)])
                    act(gkLp[b][:], gap[b][:], AF.Exp, [("gap", b)], [("gkLp", b)])
                    ts("dve", gkLp[b][:], gkLp[b][:], segm[:, seg:seg + 1], None, ALU.mult, ALU.bypass, [("gkLp", b), "segm"], [("gkLp", b)])
                    tt("dve", decp[b][:], sufp[b][:, 0, :], tot[:, 0, :], ALU.add, [("sufp", b), ("ps", pq)], [("decp", b)])
                    act(decp[b][:], decp[b][:], AF.Exp, [("decp", b)], [("decp", b)], scale=-1.0)

                carry_pending = [False]

                def stageC(u):
                    b = u % 2
                    if carry_pending[0]:
                        carry_finish()
                        carry_pending[0] = False
                    for t in range(4):
                        pi = nextps()
                        pb = ps[pi][:].bitcast(BF16)
                        for h in range(4):
                            E("pe", "transpose", pb[:, h * 128:(h + 1) * 128], kTp[b][:, h, t * 128:(t + 1) * 128], ident_b[:],
                              r=[("kTp", b), "ident_b"], w=[("ps", pi)])
                        tt("dve", khp[b][:, t, :, :], pb[:, 0:512].rearrange("p (h d) -> p h d", h=4),
                           gkLp[b][:, t, :].unsqueeze(2).to_broadcast([128, 4, 128]), ALU.mult, [("ps", pi), ("gkLp", b)], [("khp", b)])
                    Zt5 = ZtP[b][:].rearrange("p (j q e) n -> p j q e n", j=4, q=4)
                    S5C = True
                    bks = {}

                    def s5_mm(bb):
                        bk = [nextps(pin=True) for _ in range(4)]
                        bks[bb] = bk
                        for jj in range(2):
                            j = 2 * bb + jj
                            for e_ in range(2):
                                for sg in range(8):
                                    for pp in range(4):
                                        pi = bk[pp]
                                        pv = ps[pi][:].rearrange("p (w j e n) -> p w j e n", w=2, j=2, e=2)
                                        E("pe", "matmul", pv[:, 0, jj, e_, :], PWB[32 * pp:32 * pp + 32, j, e_, sg, :],
                                          uTp[b][32 * pp:32 * pp + 32, j, sg:UT:8], start=(sg == 0), stop=(sg == 7),
                                          tile_position=(32 * pp, 0), r=["PWB", ("uTp", b)], w=[("ps", pi)])

                    def s5_post(bb):
                        bk = bks[bb]
                        for pp in range(4):
                            pi = bk[pp]
                            act(Wsp[pp][:], ps[pi][:, 0:256], AF.Copy, [("ps", pi)], [("Wsp", pp)])
                            E("pe", "matmul", ps[pi][:, 256:512], rot_b[:], Wsp[pp][:], start=True, stop=True,
                              r=["rot_b", ("Wsp", pp)], w=[("ps", pi)])
                            zi = pp % 2
                            zv = ztp[zi][:].rearrange("p (j e n) -> p j e n", j=2, e=2)
                            w1v = Wsp[pp][:].rearrange("p (j e n) -> p j e n", j=2, e=2)
                            w2v = ps[pi][:, 256:512].rearrange("p (j e n) -> p j e n", j=2, e=2)
                            zo = Zt5[:, 2 * bb:2 * bb + 2, pp, :, :]
                            tt("dve", zo, Er5[:, 2 * bb:2 * bb + 2, pp, :, :], w1v, ALU.mult, ["Etab", ("Wsp", pp)], [("ZtP", b)])
                            tt("dve", zv, Ei5[:, 2 * bb:2 * bb + 2, pp, :, :], w2v, ALU.mult, ["Etab", ("ps", pi)], [("ztp", zi)])
                            tt("dve", zo, zo, zv, ALU.add, [("ZtP", b), ("ztp", zi)], [("ZtP", b)])
                        unpin(bk)

                    s5_mm(0)
                    for h in range(4):
                        pi = nextps()
                        for t in range(4):
                            E("pe", "matmul", ps[pi][:, 0:129], khp[b][:, t, h, :], vtp[b][:, t, h, :], start=(t == 0), stop=(t == 3),
                              r=[("khp", b), ("vtp", b)], w=[("ps", pi)])
                        stt("dve", Cst_f[:, h, :], Cst_f[:, h, :], decp[b][:, h:h + 1], ps[pi][:, 0:129], ALU.mult, ALU.add,
                            ["Cst_f", ("decp", b), ("ps", pi)], ["Cst_f"])
                    s5_post(0)
                    s5_mm(1)
                    s5_post(1)
                    E("dve", "reduce_sum", out=sumz[:], in_=ZtP[b][:], axis=AX.X, r=[("ZtP", b)], w=["sumz"])
                    tt("dve", send[:], Scar[:, 0, :], sumz[:], ALU.add, ["Scar", "sumz"], ["send"])
                    carry_pending[0] = True

                def carry_finish():
                    pi = nextps()
                    E("pe", "matmul", ps[pi][:, 0:32], rot, send[:], start=True, stop=True, r=["cst", "send"], w=[("ps", pi)])
                    tt("dve", cr2[:, 0, :], L5r[:, 0, :], send[:], ALU.mult, ["L5", "send"], ["cr2"])
                    tt("dve", cr2[:, 1, :], L5i[:, 0, :], ps[pi][:, 0:32], ALU.mult, ["L5", ("ps", pi)], ["cr2"])
                    tt("dve", Scar[:, 0, :], cr2[:, 0, :], cr2[:, 1, :], ALU.add, ["cr2"], ["Scar"])

                w1b_early = [sq(f"w1be{i}", [128, 8, 128], BF16) for i in range(2)]
                wsl = [0]
                n_ = len(pre_units)
                stageA1(pre_units[0])
                stageA1b(pre_units[0])
                for i in range(n_ + 2):
                    if i < n_:
                        stageA2(pre_units[i])
                    if i + 1 < n_:
                        stageA1(pre_units[i + 1])
                    if 0 <= i - 1 < n_:
                        stageB(pre_units[i - 1])
                    if i + 1 < n_:
                        stageA1b(pre_units[i + 1])
                    if 0 <= i - 2 < n_:
                        stageC(pre_units[i - 2])
                if carry_pending[0]:
                    carry_finish()
                pi = nextps()
                E("pe", "matmul", ps[pi][:, 0:32], rot, Scar[:, 0, :], start=True, stop=True, r=["cst", "Scar"], w=[("ps", pi)])
                E("dve", "tensor_copy", out=Scar[:, 1, :], in_=ps[pi][:, 0:32], r=[("ps", pi)], w=["Scar"])
                act(Cst_b[:].rearrange("p h d -> p (h d)"), Cst_f[:].rearrange("p h d -> p (h d)"), AF.Copy, ["Cst_f"], ["Cst_b"])
                E("dve", "tensor_copy", out=hal[:, 4:8, :], in_=halb[:], r=["halb"], w=["hal"])
                S.barrier()
            unit_list = own_units

        CZ = sb("CZ2", [128, 32, 9, 16], BF16)
        S.dma("sp", CZ[:].rearrange("p g d c -> p (g d c)"), CZ_d, reads=["CZ_d"], writes=["CZ"])
        xts = [sb(f"xt{i}", [128, 4, DM]) for i in range(2)]
        hT2 = sb("hT2", [128, 8, UT], BF16)
        xsq = sb("xsq", [128, DM])
        XQ = [("xq", i) for i in range(4)]
        hT = sb("hT", [128, 8, UT], BF16)
        uT = sb("uT", [128, 4, UT], BF16); osT = sb("osT", [128, 4, UT], BF16)
        hmT = sb("hmT", [128, 4, UT], BF16); y2T = sb("y2T", [128, 4, UT], BF16)
        rstats = [sb(f"rstat{i}", [128, 8]) for i in range(2)]; bstn = sb("bstn", [128, 2, 6]); bagn = sb("bagn", [128, 2])
        gat = sb("gat", [128, 4, 8]); spl = sb("spl", [128, 4, 4])
        wq = sb("wq", [128, 4, 4]); ga = sb("ga", [128, 4, 4]); gk = sb("gk", [128, 4, 4]); gkL = sb("gkL", [128, 4, 4])
        dec = sb("dec", [128, 4, 4])
        Sms = [sb(f"Sm{i}", [128, 4, 128], BF16) for i in range(2)]
        khats = [sb(f"khat{i}", [128, 4, 128], BF16) for i in range(2)]
        eps_ = [sb(f"ep{i}", [128, 10, 4]) for i in range(2)]
        hmtms = [sb(f"hmtm{i}", [128, 4, 128], BF16) for i in range(2)]
        hsq = sb("hsq", [128, 4, 128])
        W1s = [sb(f"W1s{i}", [128, 256], BF16) for i in range(2)]
        W2ss = [sb(f"W2ss{i}", [128, 256], BF16) for i in range(2)]
        ztp = sb("ztp", [128, 256]); ztq = sb("ztq", [128, 256]); cry = sb("cry", [128, 4, 32])
        wring = [sb(f"wring{i}", [128, DM], BF16) for i in range(3)]
        NWR = 3
        mask01 = sb("mask01", [128, 32, 65], BF16)
        E("pool", "memset", mask01[:], 1.0, w=["mask01"])
        E("pool", "memset", mask01[:, :, 0:1], 0.0, w=["mask01"])
        arena1 = sb("arena1", [128, 8320])
        Zaug = arena1[:, 0:2080].rearrange("p (g n) -> p g n", g=32)
        Zaug2 = arena1[:, 2080:4160].rearrange("p (g n) -> p g n", g=32)
        Ssc = arena1[:, 4160:6240].rearrange("p (g n) -> p g n", g=32)
        Ssc2 = arena1[:, 6240:8320].rearrange("p (g n) -> p g n", g=32)
        Ybm5 = arena1[0:64, 0:2048].bitcast(BF16).rearrange("p (j t g c) -> p j t g c", j=4, t=8, g=8)
        aT = arena1[:, 0:8192].bitcast(BF16).rearrange("p (c t) -> p c t", c=32)
        PWTf = arena1[:, 4160:8256].bitcast(BF16)
        PWT = PWTf.rearrange("p (j e s m) -> p j e s m", j=4, e=2, s=8)
        for k_ in ["Zaug", "Zaug2", "Ssc", "Ssc2", "Ybm", "aT", "PWT"]:
            S.alias[k_] = "arena1"
        arena2 = sb("arena2", [128, 5136])
        a2b = arena2[:].bitcast(BF16)
        xnr2 = [a2b[:, 0:1024], a2b[:, 1024:2048]]
        qT = a2b[:, 2048:4096].rearrange("p (c t) -> p c t", c=4)
        kT = a2b[:, 4096:6144].rearrange("p (c t) -> p c t", c=4)
        pre = arena2[:, 3072:3072 + UT + 3]
        cacc = arena2[:, 3590:3590 + UT]
        vtm = a2b[:, 8204:8204 + 2064].rearrange("p (t h d) -> p t h d", t=4, h=4)
        zT = a2b[:, 0:2048].rearrange("p (c t) -> p c t", c=4)
        sgT = a2b[:, 2048:4096].rearrange("p (c t) -> p c t", c=4)
        Xb = a2b[:, 4096:6144].rearrange("p (g n) -> p g n", g=32)
        Yblk = a2b[:, 6144:8192].rearrange("p (g n) -> p g n", g=32)
        for k_ in ["xn", "qk", "pre", "cacc", "vtm", "zT", "sgT", "Xb", "Yblk"]:
            S.alias[k_] = "arena2"
        wrc = [0]
        pending_tail = []

        def wslot():
            i = wrc[0] % NWR
            wrc[0] += 1
            return i

        def norm_stats(xt, kx, rstat, kr):
            for t in range(4):
                E("dve", "bn_stats", out=bstn[:, 0, :], in_=xt[:, t, 0:512], r=[kx], w=["bstn"])
                E("dve", "bn_stats", out=bstn[:, 1, :], in_=xt[:, t, 512:1024], r=[kx], w=["bstn"])
                E("dve", "bn_aggr", out=bagn[:], in_=bstn[:].rearrange("p a b -> p (a b)"), r=["bstn"], w=["bagn"])
                stt("dve", rstat[:, t:t + 1], bagn[:, 0:1], bagn[:, 0:1], bagn[:, 1:2], ALU.mult, ALU.add, ["bagn"], [kr])
            ts("dve", rstat[:, 4:8], rstat[:, 0:4], EPS, None, ALU.add, ALU.bypass, [kr], [kr])
            act(rstat[:, 4:8], rstat[:, 4:8], AF.Sqrt, [kr], [kr])
            E("dve", "reciprocal", out=rstat[:, 4:8], in_=rstat[:, 4:8], r=[kr], w=[kr])

        def norm_T(xt, kx, rstat, kr, gcol, tag, hdst, kh):
            pbs = []
            for t in range(4):
                ni = t % 2
                act(xnr2[ni], xt[:, t, :], AF.Copy, [kx, kr], ["xn"], scale=rstat[:, 4 + t:5 + t])
                if t % 2 == 0:
                    pbs = [nextps(), nextps()]
                for c in range(8):
                    pi = pbs[c // 4]
                    pb = ps[pi][:].bitcast(BF16)
                    col = ((c % 4) * 2 + (t % 2)) * 128
                    E("pe", "transpose", pb[:, col:col + 128], xnr2[ni][:, c * 128:(c + 1) * 128], ident_b[:],
                      r=["xn", "ident_b"], w=[("ps", pi)])
                if t % 2 == 1:
                    for half in range(2):
                        pi = pbs[half]
                        pb = ps[pi][:].bitcast(BF16)
                        for cc in range(4):
                            c = half * 4 + cc
                            dst = hdst[:, c, (t - 1) * 128:(t + 1) * 128]
                            src = pb[:, cc * 256:(cc + 1) * 256]
                            if half == 0:
                                act(dst, src, AF.Copy, [("ps", pi), tag], [(kh, c)], scale=gcol[:, c:c + 1])
                            else:
                                ts("dve", dst, src, gcol[:, c:c + 1], None, ALU.mult, ALU.bypass, [("ps", pi), tag], [(kh, c)])

        def proj_fm(col0, nchunks, evac):
            for c in range(nchunks):
                bi = wslot()
                wb = wring[bi][:].rearrange("p (k m) -> p k m", k=8)
                S.dma("sp", wb, win_s[:, :, col0 + c * 128:col0 + (c + 1) * 128], reads=["win_s"], writes=[("wr", bi)])
                pi = nextps()
                for kc in range(8):
                    E("pe", "matmul", ps[pi][:], wb[:, kc, :], hT[:, kc, :], start=(kc == 0), stop=(kc == 7),
                      r=[("wr", bi), ("hT", kc)], w=[("ps", pi)])
                evac(c, pi)

        def conv_chunk(c, pi, dstT):
            E("pool", "tensor_copy", out=pre[:, 0:3], in_=hal[:, c, :], r=["hal"], w=["pre"])
            act(pre[:, 3:UT + 3], ps[pi][:], AF.Copy, [("ps", pi)], ["pre"])
            E("pool", "tensor_copy", out=hal[:, c, :], in_=pre[:, UT:UT + 3], r=["pre"], w=["hal"])
            ts("dve", cacc, pre[:, 0:UT], cw[:, c, 0:1], cb[:, c:c + 1], ALU.mult, ALU.add, ["pre", "cw", "cb"], ["cacc"])
            for jj in range(1, 4):
                stt("dve", cacc, pre[:, jj:jj + UT], cw[:, c, jj:jj + 1], cacc, ALU.mult, ALU.add,
                    ["pre", "cw", "cacc"], ["cacc"])
            act(dstT, cacc, AF.Silu, ["cacc"], [("qk", c)])

        Er5 = Er[:].rearrange("p (j q e) n -> p j q e n", j=4, q=4)
        Ei5 = Ei[:].rearrange("p (j q e) n -> p j q e n", j=4, q=4)
        Za5 = Zaug.rearrange("p (j q e) n -> p j q e n", j=4, q=4)
        Zb5 = Zaug2.rearrange("p (j q e) n -> p j q e n", j=4, q=4)

        def partA(u):
            seg = u // 4
            tok0 = u * UT
            b = u % 2
            xt = xts[b]; kx = ("xt", b); rstat = rstats[b]; kr = ("rstat", b)
            for t in range(4):
                S.dma("sp", xt[:, t, :], xall[tok0 + t * 128: tok0 + (t + 1) * 128, :], writes=[kx])
            norm_stats(xt, kx, rstat, kr)
            norm_T(xt, kx, rstat, kr, g1, "g1", hT, "hT")
            proj_fm(0, 4, lambda c, pi: conv_chunk(c, pi, qT[:, c, :]))
            proj_fm(512, 4, lambda c, pi: conv_chunk(4 + c, pi, kT[:, c, :]))
            proj_fm(2056, 4, lambda c, pi: act(uT[:, c, :], ps[pi][:], AF.Copy, [("ps", pi)], ["uT"]))
            proj_fm(1536, 4, lambda c, pi: act(osT[:, c, :], ps[pi][:], AF.Sigmoid, [("ps", pi)], ["osT"]))
            E("pool", "memset", vtm[:, :, :, 128:129], 1.0, w=["vtm"])
            pg = nextps()
            for t in range(4):
                pi = nextps()
                for kc in range(8):
                    E("pe", "matmul", ps[pi][:], hT[:, kc, t * 128:(t + 1) * 128], wvg[:, kc, 0:512], start=(kc == 0), stop=(kc == 7),
                      r=["wvg", ("hT", kc)], w=[("ps", pi)])
                act(vtm[:, t, :, 0:128], ps[pi][:].rearrange("p (h d) -> p h d", h=4), AF.Copy, [("ps", pi)], ["vtm"])
                for kc in range(8):
                    E("pe", "matmul", ps[pg][:, t * 8:(t + 1) * 8], hT[:, kc, t * 128:(t + 1) * 128], wvg[:, kc, 512:520],
                      start=(kc == 0), stop=(kc == 7), r=["wvg", ("hT", kc)], w=[("ps", pg)])
            tt("dve", gat[:], ps[pg][:, 0:32].rearrange("p (t g) -> p t g", t=4), gbias[:].unsqueeze(1).to_broadcast([128, 4, 8]),
               ALU.add, [("ps", pg), "gbias"], ["gat"])
            act(spl[:], gat[:, :, 4:8], AF.Exp, ["gat"], ["spl"], scale=-1.0)
            act(spl[:], spl[:], AF.Ln, ["spl"], ["spl"], bias=1.0)
            pq = nextps()
            sp2 = spl[:].rearrange("p t h -> p (t h)")
            E("pe", "matmul", ps[pq][:, 0:16], tri[:], sp2, start=True, stop=True, r=["tri", "spl"], w=[("ps", pq)])
            E("pe", "matmul", ps[pq][:, 16:32], ones_f[:], sp2, start=True, stop=True, r=["ones_f", "spl"], w=[("ps", pq)])
            cums = ps[pq][:, 0:16].rearrange("p (t h) -> p t h", t=4)
            tot = ps[pq][:, 16:32].rearrange("p (t h) -> p t h", t=4)
            G = [("ps", pq), "gat"]
            act(wq[:], cums, AF.Exp, G, ["wq"], scale=-1.0, bias=-0.5 * math.log(128.0))
            tt("dve", ga[:], gat[:, :, 0:4], cums, ALU.add, G, ["ga"])
            act(gk[:], ga[:], AF.Exp, ["ga"], ["gk"])
            tt("dve", ga[:], ga[:], tot, ALU.subtract, G + ["ga"], ["ga"])
            act(gkL[:], ga[:], AF.Exp, ["ga"], ["gkL"])
            ts("dve", gkL[:], gkL[:], segm[:, seg:seg + 1], None, ALU.mult, ALU.bypass, ["gkL", "segm"], ["gkL"])
            act(dec[:], tot, AF.Exp, G, ["dec"], scale=-1.0)

        def partB(u):
            b = u % 2
            xt = xts[b]; kx = ("xt", b)


            s5bk = {}

            def s5_mm(bb):
                bk = [nextps(pin=True) for _ in range(4)]
                s5bk[bb] = bk
                for jj in range(2):
                    j = 2 * bb + jj
                    for e_ in range(2):
                        for sg in range(8):
                            for pp in range(4):
                                pi = bk[pp]
                                pv = ps[pi][:].rearrange("p (w j e n) -> p w j e n", w=2, j=2, e=2)
                                E("pe", "matmul", pv[:, 0, jj, e_, :], PWB[32 * pp:32 * pp + 32, j, e_, sg, :],
                                  uT[32 * pp:32 * pp + 32, j, sg:UT:8], start=(sg == 0), stop=(sg == 7),
                                  tile_position=(32 * pp, 0), r=["PWB", "uT"], w=[("ps", pi)])

            def s5_post(bb):
                bk = s5bk[bb]
                for pp in range(4):
                    pi = bk[pp]
                    ri = pp % 2
                    W1 = W1s[ri]
                    W2s = W2ss[ri]
                    k1, k2 = ("W1", ri), ("W2s", ri)
                    act(W1[:], ps[pi][:, 0:256], AF.Copy, [("ps", pi)], [k1])
                    E("pe", "matmul", ps[pi][:, 256:512], rot_b[:], W1[:], start=True, stop=True, r=["rot_b", k1], w=[("ps", pi)])
                    act(W2s[:], ps[pi][:, 256:512], AF.Copy, [("ps", pi)], [k2])
                    w1v = W1[:].rearrange("p (j e n) -> p j e n", j=2, e=2)
                    w2v = W2s[:].rearrange("p (j e n) -> p j e n", j=2, e=2)
                    zv = ztp[:].rearrange("p (j e n) -> p j e n", j=2, e=2)
                    zq = ztq[:].rearrange("p (j e n) -> p j e n", j=2, e=2)
                    js = slice(2 * bb, 2 * bb + 2)
                    zo = Za5[:, js, pp, :, 1:65]
                    zo2 = Zb5[:, js, pp, :, 1:65]
                    tt("dve", zo, Er5[:, js, pp, :, :], w1v, ALU.mult, ["Etab", k1], ["Zaug"])
                    tt("dve", zv, Ei5[:, js, pp, :, :], w2v, ALU.mult, ["Etab", k2], ["ztp"])
                    tt("dve", zo, zo, zv, ALU.add, ["Zaug", "ztp"], ["Zaug"])
                    tt("pool", zo2, Er5[:, js, pp, :, :], w2v, ALU.mult, ["Etab", k2], ["Zaug2"])
                    tt("pool", zq, Ei5[:, js, pp, :, :], w1v, ALU.mult, ["Etab", k1], ["ztq"])
                    tt("pool", zo2, zo2, zq, ALU.subtract, ["Zaug2", "ztq"], ["Zaug2"])
                unpin(bk)

            Smfs = [xsq[:, 0:512].rearrange("p (h d) -> p h d", h=4), xsq[:, 512:1024].rearrange("p (h d) -> p h d", h=4)]
            KSm = [[XQ[0], XQ[1]], [XQ[2], XQ[3]]]
            st = {}

            def P1(t):
                par = t % 2
                tc_ = slice(t * 128, (t + 1) * 128)
                pi = nextps()
                for h in range(4):
                    E("pe", "matmul", ps[pi][:, h * 128:(h + 1) * 128], kT[:, h, tc_], qT[:, h, tc_], start=True, stop=True,
                      r=[("qk", 4 + h), ("qk", h)], w=[("ps", pi)])
                pk = nextps()
                pkb = ps[pk][:].bitcast(BF16)
                for h in range(4):
                    E("pe", "transpose", pkb[:, h * 128:(h + 1) * 128], kT[:, h, tc_], ident_b[:], r=[("qk", 4 + h), "ident_b"], w=[("ps", pk)])
                tt("dve", Smfs[par], ps[pi][:].rearrange("p (h d) -> p h d", h=4), gk[:, t, :].unsqueeze(2).to_broadcast([128, 4, 128]),
                   ALU.mult, [("ps", pi), "gk"], KSm[par])
                tt("pool", Sms[par][:], Smfs[par], trimask[:].unsqueeze(1).to_broadcast([128, 4, 128]), ALU.mult, KSm[par] + ["trimask"], [("Sm", par)])
                tt("dve", khats[par][:], pkb[:, 0:512].rearrange("p (h d) -> p h d", h=4), gkL[:, t, :].unsqueeze(2).to_broadcast([128, 4, 128]),
                   ALU.mult, [("ps", pk), "gkL"], [("khat", par)])

            def P2(t):
                par = t % 2
                tc_ = slice(t * 128, (t + 1) * 128)
                pn = [nextps(pin=True), nextps(pin=True)]
                st[t] = pn
                for h in range(4):
                    o_ = ps[pn[h // 2]][:, (h % 2) * 129:(h % 2) * 129 + 129]
                    E("pe", "matmul", o_, Sms[par][:, h, :], vtm[:, t, h, :], start=True, stop=False, r=[("Sm", par), "vtm"], w=[("ps", pn[h // 2])])
                    E("pe", "matmul", o_, qT[:, h, tc_], Cst_b[:, h, :], start=False, stop=True, r=[("qk", h), "Cst_b"], w=[("ps", pn[h // 2])])
                pc = [nextps(), nextps()]
                for h in range(4):
                    E("pe", "matmul", ps[pc[h // 2]][:, (h % 2) * 129:(h % 2) * 129 + 129], khats[par][:, h, :], vtm[:, t, h, :],
                      start=True, stop=True, r=[("khat", par), "vtm"], w=[("ps", pc[h // 2])])
                for hp in range(2):
                    cs = Cst_f[:, 2 * hp:2 * hp + 2, :]
                    tt("dve", cs, cs, dec[:, t, 2 * hp:2 * hp + 2].unsqueeze(2).to_broadcast([128, 2, 129]), ALU.mult, ["Cst_f", "dec"], ["Cst_f"])
                    tt("dve", cs, cs, ps[pc[hp]][:, 0:258].rearrange("p (h d) -> p h d", h=2), ALU.add, ["Cst_f", ("ps", pc[hp])], ["Cst_f"])
                act(Cst_b[:].rearrange("p h d -> p (h d)"), Cst_f[:].rearrange("p h d -> p (h d)"), AF.Copy, ["Cst_f"], ["Cst_b"])

            def P3(t):
                par = t % 2
                pn = st[t]
                e_ = eps_[par]
                ke = ("ep", par)
                PN = [("ps", pn[0]), ("ps", pn[1])]
                for hp in range(2):
                    act(e_[:, 0, 2 * hp:2 * hp + 2], ps[pn[hp]][:, 128:258:129], AF.Abs, [PN[hp]], [ke])
                tt("dve", e_[:, 1, :], e_[:, 0, :], wq[:, t, :], ALU.mult, [ke, "wq"], [ke])
                ts("dve", e_[:, 1, :], e_[:, 1, :], 1.0, None, ALU.max, ALU.bypass, [ke], [ke])
                E("dve", "reciprocal", out=e_[:, 2, :], in_=e_[:, 1, :], r=[ke], w=[ke])
                tt("dve", e_[:, 3, :], e_[:, 2, :], wq[:, t, :], ALU.mult, [ke, "wq"], [ke])
                hv4 = Smfs[par]
                for hp in range(2):
                    tt("dve", hv4[:, 2 * hp:2 * hp + 2, :], ps[pn[hp]][:, 0:258].rearrange("p (h d) -> p h d", h=2)[:, :, 0:128],
                       e_[:, 3, 2 * hp:2 * hp + 2].unsqueeze(2).to_broadcast([128, 2, 128]), ALU.mult, [PN[hp], ke, ("Sm", par)], KSm[par])
                E("dve", "reduce_sum", out=e_[:, 4, :], in_=hv4, axis=AX.X, r=KSm[par], w=[ke])
                tt("pool", hsq[:], hv4, hv4, ALU.mult, KSm[par], ["hsq"])
                E("dve", "reduce_sum", out=e_[:, 5, :], in_=hsq[:], axis=AX.X, r=["hsq"], w=[ke])
                ts("dve", e_[:, 6, :], e_[:, 4, :], 1.0 / 128.0, None, ALU.mult, ALU.bypass, [ke], [ke])
                tt("dve", e_[:, 7, :], e_[:, 6, :], e_[:, 6, :], ALU.mult, [ke], [ke])
                stt("dve", e_[:, 8, :], e_[:, 5, :], 1.0 / 128.0, e_[:, 7, :], ALU.mult, ALU.subtract, [ke], [ke])
                ts("dve", e_[:, 8, :], e_[:, 8, :], EPS, None, ALU.add, ALU.bypass, [ke], [ke])
                act(e_[:, 8, :], e_[:, 8, :], AF.Sqrt, [ke], [ke])
                E("dve", "reciprocal", out=e_[:, 9, :], in_=e_[:, 8, :], r=[ke], w=[ke])
                tt("dve", hv4, hv4, e_[:, 6, :].unsqueeze(2).to_broadcast([128, 4, 128]), ALU.subtract, KSm[par] + [ke], KSm[par])
                tt("dve", hmtms[par][:], hv4, e_[:, 9, :].unsqueeze(2).to_broadcast([128, 4, 128]), ALU.mult, KSm[par] + [ke], [("hmtm", par)])
                unpin(pn)

            def P4(t):
                par = t % 2
                tc_ = slice(t * 128, (t + 1) * 128)
                pt_ = nextps()
                ptb = ps[pt_][:].bitcast(BF16)
                for h in range(4):
                    E("pe", "transpose", ptb[:, h * 128:(h + 1) * 128], hmtms[par][:, h, :], ident_b[:], r=[("hmtm", par), "ident_b"], w=[("ps", pt_)])
                tt("dve", hmT[:, :, tc_], ptb[:, 0:512].rearrange("p (h d) -> p h d", h=4), mnwc[:].unsqueeze(2).to_broadcast([128, 4, 128]),
                   ALU.mult, [("ps", pt_), "mnwc"], ["hmT"])
                tt("pool", hmT[:, :, tc_], hmT[:, :, tc_], osT[:, :, tc_], ALU.mult, ["hmT", "osT"], ["hmT"])

            P1(0)
            s5_mm(0)
            P2(0)
            P1(1)
            s5_post(0)
            P3(0)
            s5_mm(1)
            P2(1)
            P1(2)
            s5_post(1)
            P3(1)
            P4(0)
            E("dve", "tensor_copy", out=Zaug[:, :, 0], in_=Scar[:, 0, :], r=["Scar"], w=["Zaug"])
            E("pool", "tensor_copy", out=Zaug2[:, :, 0], in_=Scar[:, 1, :], r=["Scar"], w=["Zaug2"])
            m01 = mask01[:].rearrange("p g n -> p (g n)")
            E("dve", "tensor_tensor_scan", out=Ssc.rearrange("p g n -> p (g n)"), data0=m01, data1=Zaug.rearrange("p g n -> p (g n)"),
              initial=0.0, op0=ALU.mult, op1=ALU.add, r=["Zaug", "mask01"], w=["Ssc"])
            E("dve", "tensor_tensor_scan", out=Ssc2.rearrange("p g n -> p (g n)"), data0=m01, data1=Zaug2.rearrange("p g n -> p (g n)"),
              initial=0.0, op0=ALU.mult, op1=ALU.add, r=["Zaug2", "mask01"], w=["Ssc2"])
            S.dma("sp", Zaug[:, :, 0:64], Fr_d.rearrange("p (g n) -> p g n", g=32), reads=["Fr_d", "Zaug"], writes=["Zaug"])
            S.dma("sp", Zaug2[:, :, 0:64], Fi_d.rearrange("p (g n) -> p g n", g=32), reads=["Fi_d", "Zaug2"], writes=["Zaug2"])
            tt("dve", Zaug[:, :, 0:64], Zaug[:, :, 0:64], Ssc[:, :, 0:64], ALU.mult, ["Zaug", "Ssc"], ["Zaug"])
            tt("pool", Zaug2[:, :, 0:64], Zaug2[:, :, 0:64], Ssc2[:, :, 0:64], ALU.mult, ["Zaug2", "Ssc2"], ["Zaug2"])
            l5r = L5r[:, 0, :]; l5i = L5i[:, 0, :]
            tt("dve", cry[:, 0, :], l5r, Ssc[:, :, 64], ALU.mult, ["L5", "Ssc"], ["cry"])
            tt("dve", cry[:, 1, :], l5i, Ssc2[:, :, 64], ALU.mult, ["L5", "Ssc2"], ["cry"])
            tt("dve", cry[:, 2, :], l5r, Ssc2[:, :, 64], ALU.mult, ["L5", "Ssc2"], ["cry"])
            tt("dve", cry[:, 3, :], l5i, Ssc[:, :, 64], ALU.mult, ["L5", "Ssc"], ["cry"])
            tt("dve", Scar[:, 0, :], cry[:, 0, :], cry[:, 1, :], ALU.add, ["cry"], ["Scar"])
            tt("dve", Scar[:, 1, :], cry[:, 2, :], cry[:, 3, :], ALU.subtract, ["cry"], ["Scar"])
            S.dma("sp", PWTf, PWT_d, reads=["PWT_d", "Ssc", "Ssc2"], writes=["PWT"])
            P2(2)
            P1(3)
            P3(2)
            P4(1)
            P2(3)
            tt("dve", Xb, Zaug[:, :, 0:64], Zaug2[:, :, 0:64], ALU.add, ["Zaug", "Zaug2"], ["Xb"])
            for j in range(4):
                bk = [nextps() for _ in range(4)]
                for e_ in range(2):
                    for sg in range(8):
                        for pp in range(4):
                            pi = bk[pp]
                            pv = ps[pi][:, 0:128].rearrange("p (g n) -> p g n", g=2)
                            E("pe", "matmul", pv[:, e_, :], PWT[32 * pp:32 * pp + 32, j, e_, sg, :],
                              uT[32 * pp:32 * pp + 32, j, sg:UT:8], start=(sg == 0), stop=False,
                              tile_position=(32 * pp, 0), r=["PWT", "uT"], w=[("ps", pi)])
                    for pp in range(4):
                        pi = bk[pp]
                        pv = ps[pi][:, 0:128].rearrange("p (g n) -> p g n", g=2)
                        g = 8 * j + 2 * pp + e_
                        E("pe", "matmul", pv[:, e_, :], CZ[:, g, 1:9, :].rearrange("p d c -> p (d c)"), Xb[:, g, :],
                          start=False, stop=True, r=["CZ", "Xb"], w=[("ps", pi)])
                for pp in range(4):
                    pi = bk[pp]
                    g0 = 8 * j + 2 * pp
                    act(Yblk[:, g0:g0 + 2, :], ps[pi][:, 0:128].rearrange("p (g n) -> p g n", g=2), AF.Copy, [("ps", pi)], ["Yblk"])
            P3(3)
            P4(2)
            P4(3)
            if pending_tail:
                pending_tail.pop(0)()
            for q4 in range(8):
                pi = nextps()
                pb = ps[pi][:].bitcast(BF16)
                for gi in range(4):
                    g = q4 * 4 + gi
                    E("pe", "transpose", pb[0:64, gi * 128:(gi + 1) * 128], Yblk[:, g, :], ident_b[:], r=["Yblk", "ident_b"], w=[("ps", pi)])
                E("dve" if q4 % 2 == 0 else "act", "tensor_copy" if q4 % 2 == 0 else "copy",
                  out=Ybm5[:, q4 // 2, :, 4 * (q4 % 2):4 * (q4 % 2) + 4, :],
                  in_=pb[0:64, 0:512].rearrange("p (g t c) -> p t g c", g=4, t=8), r=[("ps", pi)], w=["Ybm"])
            for j in range(4):
                pi = nextps()
                pb = ps[pi][:].bitcast(BF16)
                for tau in range(8):
                    E("pe", "transpose", pb[:, tau * 64:(tau + 1) * 64], Ybm5[:, j, tau, :, :].rearrange("p g c -> p (g c)"), ident_b[0:64, 0:64],
                      r=["Ybm", "ident_b"], w=[("ps", pi)])
                act(zT[:, j, :].rearrange("p (n t) -> p t n", t=8), pb[:, 0:512].rearrange("p (t n) -> p t n", t=8), AF.Gelu,
                    [("ps", pi)], ["zT"])
            for jo in range(4):
                bi = wslot()
                wb = wring[bi][:, 0:512].rearrange("p (k m) -> p k m", k=4)
                S.dma("pool", wb, d_gluw[:, :, jo * 128:(jo + 1) * 128], writes=[("wr", bi)])
                pi = nextps()
                for ji in range(4):
                    E("pe", "matmul", ps[pi][:], wb[:, ji, :], zT[:, ji, :], start=(ji == 0), stop=(ji == 3),
                      r=[("wr", bi), "zT"], w=[("ps", pi)])
                act(sgT[:, jo, :], ps[pi][:], AF.Sigmoid, [("ps", pi), "glub"], ["sgT"], bias=glub[:, jo:jo + 1])
            tt("dve", y2T[:], zT, sgT, ALU.mult, ["zT", "sgT"], ["y2T"])
            pis = [nextps() for _ in range(8)]
            for kc in range(8):
                bi = wslot()
                wb = wring[bi]
                S.dma("pool", wb[:], d_wout[:, kc, :], writes=[("wr", bi)])
                for t in range(4):
                    tc_ = slice(t * 128, (t + 1) * 128)
                    lh = hmT[:, kc, tc_] if kc < 4 else y2T[:, kc - 4, tc_]
                    for hf in range(2):
                        pi = pis[t * 2 + hf]
                        E("pe", "matmul", ps[pi][:], lh, wb[:, hf * 512:(hf + 1) * 512], start=(kc == 0), stop=(kc == 7),
                          r=["hmT", "y2T", ("wr", bi)], w=[("ps", pi)])
            for t in range(4):
                for hf in range(2):
                    pi = pis[t * 2 + hf]
                    tt("dve", xt[:, t, hf * 512:(hf + 1) * 512], xt[:, t, hf * 512:(hf + 1) * 512], ps[pi][:], ALU.add,
                       [kx, ("ps", pi)], [kx])

        def partC(u):
            b = u % 2
            xt = xts[b]; kx = ("xt", b); rstat = rstats[b]; kr = ("rstat", b)
            norm_stats(xt, kx, rstat, kr)
            norm_T(xt, kx, rstat, kr, g2, "g2", hT2, "hT2")
            for fc in range(32):
                bi = wslot()
                wb = wring[bi][:].rearrange("p (k m) -> p k m", k=8)
                S.dma("sp", wb, wff1_s[fc], reads=[("wff1_s", fc)], writes=[("wr", bi)])
                pi = nextps()
                for kc in range(8):
                    E("pe", "matmul", ps[pi][:], wb[:, kc, :], hT2[:, kc, :], start=(kc == 0), stop=(kc == 7),
                      r=[("wr", bi), ("hT2", kc)], w=[("ps", pi)])
                ri = fc % 2
                rl = xsq[:, ri * 512:(ri + 1) * 512]
                act(rl, ps[pi][:], AF.Relu, [("ps", pi)], [XQ[2 * ri], XQ[2 * ri + 1]])
                tt("pool", aT[:, fc, :], rl, rl, ALU.mult, [XQ[2 * ri], XQ[2 * ri + 1]], ["aT"])
            pis = [nextps() for _ in range(8)]
            for fc in range(32):
                bi = wslot()
                wb = wring[bi]
                S.dma("sp", wb[:], wff2_s[fc], reads=[("wff2_s", fc)], writes=[("wr", bi)])
                for t in range(4):
                    for hf in range(2):
                        pi = pis[t * 2 + hf]
                        E("pe", "matmul", ps[pi][:], aT[:, fc, t * 128:(t + 1) * 128], wb[:, hf * 512:(hf + 1) * 512],
                          start=(fc == 0), stop=(fc == 31), r=["aT", ("wr", bi)], w=[("ps", pi)])
            for t in range(4):
                for hf in range(2):
                    pi = pis[t * 2 + hf]
                    tt("dve", xt[:, t, hf * 512:(hf + 1) * 512], xt[:, t, hf * 512:(hf + 1) * 512], ps[pi][:], ALU.add,
                       [kx, ("ps", pi)], [kx])
            def tail():
                norm_stats(xt, kx, rstat, kr)
                for t in range(4):
                    act(xsq[:], xt[:, t, :], AF.Copy, [kx, kr], XQ, scale=rstat[:, 4 + t:5 + t])
                    tt("pool", xsq[:], xsq[:], g3[:], ALU.mult, XQ + ["g3"], XQ)
                    r0 = (u - FIRST_OWN) * UT + t * 128
                    S.dma("sp", out[r0:r0 + 128, :], xsq[:], reads=XQ, is_output=True)
            pending_tail.append(tail)


        n_own = len(unit_list)
        if n_own:
            partA(unit_list[0])
        for i in range(n_own):
            partB(unit_list[i])
            if i + 1 < n_own:
                partA(unit_list[i + 1])
            partC(unit_list[i])
        while pending_tail:
            pending_tail.pop(0)()
        if dbg_sel is not None:
            S.dma("sp", dbg2[:, 32760:32768], rstats[0][:], reads=[("rstat", 0)], is_output=True)

        with nc.Block() as block:
            S.finish(block)
    return nc


def _host_layout(inp, core):
    f = np.float32
    x = np.asarray(inp["x"], f)[0]
    d = {}
    xall = np.zeros((NSEG * TOK, DM), f)
    n_real = (core + 1) * TOK
    xall[NSEG * TOK - n_real:] = x[:n_real]
    d["xall"] = xall
    segm = np.zeros((128, 8), f)
    segm[:, NSEG - (core + 1):] = 1.0
    d["segm"] = segm
    return d


def _host_shared(inp):
    f = np.float32
    g = lambda k: np.asarray(inp[k], f)
    d = {}
    d["w_in"] = np.ascontiguousarray(g("w_in")[0].reshape(8, 128, INC).transpose(1, 0, 2))
    d["w_out"] = np.ascontiguousarray(g("w_out")[0].reshape(8, 128, DM).transpose(1, 0, 2))
    d["glu_w"] = np.ascontiguousarray(g("glu_w")[0].reshape(4, 128, 512).transpose(1, 0, 2))
    d["w_ff1"] = np.ascontiguousarray(g("w_ff1")[0].reshape(8, 128, 32, 128).transpose(2, 1, 0, 3))
    d["w_ff2"] = np.ascontiguousarray(g("w_ff2")[0].reshape(32, 128, DM))
    d["cw"] = np.ascontiguousarray(g("conv_w")[0].reshape(4, 8, 128).transpose(2, 1, 0))
    d["cb"] = np.ascontiguousarray(g("conv_b")[0].reshape(8, 128).T)
    d["gbias"] = np.ascontiguousarray(np.broadcast_to(np.concatenate([g("i_bias")[0], g("f_bias")[0]])[None, :], (128, 8)))
    d["mnwc"] = np.ascontiguousarray(g("mlstm_norm_w")[0].reshape(4, 128).T)
    d["g1"] = np.ascontiguousarray(g("mix_norm_w")[0].reshape(8, 128).T)
    d["g2"] = np.ascontiguousarray(g("mlp_norm_w")[0].reshape(8, 128).T)
    d["g3"] = np.ascontiguousarray(np.broadcast_to(g("final_norm_w")[None, :], (128, DM)))
    d["glub"] = np.ascontiguousarray(g("glu_b")[0].reshape(4, 128).T)
    lr, li, ld = g("ssm_lam_re")[0], g("ssm_lam_im")[0], g("ssm_log_dt")[0]
    d["lr2"] = np.ascontiguousarray(np.tile(lr.T, (2, 1)))
    d["li2"] = np.ascontiguousarray(np.tile(li.T, (2, 1)))
    d["ld2"] = np.ascontiguousarray(np.broadcast_to(ld[None, :], (128, 32)))

    def tl1(a):
        a4 = a.reshape(4, 8, 64)
        o = np.broadcast_to(a4.transpose(1, 0, 2)[:, None, :, :], (8, 16, 4, 64))
        return np.ascontiguousarray(o.reshape(128, 256))
    d["lr1"] = tl1(lr); d["li1"] = tl1(li)
    d["ld1"] = tl1(np.broadcast_to(ld[:, None], (32, 64)))
    br, bi = g("ssm_b_re")[0], g("ssm_b_im")[0]
    cr, ci = g("ssm_c_re")[0], g("ssm_c_im")[0]
    bpr = br.transpose(1, 0, 2).reshape(64, 512); bpi = bi.transpose(1, 0, 2).reshape(64, 512)
    d["AB"] = np.ascontiguousarray(np.concatenate([bpr, bpi], 0))
    d["ABs"] = np.ascontiguousarray(np.concatenate([bpi, bpr], 0))
    cpr = cr.transpose(2, 0, 1).reshape(64, 512); cpi = ci.transpose(2, 0, 1).reshape(64, 512)
    d["AC"] = np.ascontiguousarray(np.concatenate([cpr, cpi], 0))
    d["ACs"] = np.ascontiguousarray(np.concatenate([cpi, cpr], 0))

    def tl1b(b):
        b4 = b.reshape(4, 8, 64, 16)
        return np.ascontiguousarray(b4.transpose(1, 3, 0, 2).reshape(128, 256))
    d["bT1r"] = tl1b(br); d["bT1i"] = tl1b(bi)
    D = g("ssm_d")[0].reshape(4, 8, 16)
    dpad = np.zeros((8, 16, 4, 16), f)
    for c in range(16):
        dpad[:, c, :, c] = D[:, :, c].T
    d["dpad"] = np.ascontiguousarray(dpad.reshape(128, 4, 16))
    cst = np.zeros((128, 384), f)
    cst[:, 0:9] = np.arange(9)
    n = np.arange(64)
    cst[:, 16:80] = -8.0 * (n - 32)
    cst[:, 80:144] = 8.0 * (n - 32)
    cst[:, 144:152] = -(np.arange(8) + 1.0)
    cst[:64, 152] = -1.0; cst[64:, 152] = 1.0
    cst[:64, 153] = 1.0; cst[64:, 153] = -1.0
    q = np.arange(128)
    cst[:, 154] = ((q // 16) % 2 == 0); cst[:, 155] = ((q // 16) % 2 == 1)
    for g8 in range(8):
        cst[:, 156 + g8] = (q // 16 == g8)
    cst[:, 164] = 512.0
    cst[:, 165] = 1.0
    rot = np.zeros((128, 128), f)
    for m in range(64):
        rot[m + 64, m] = -1.0
        rot[m, m + 64] = 1.0
    cst[:, 256:384] = rot
    d["cst"] = cst
    d["ident"] = np.eye(128, dtype=f)
    d["tri"] = np.triu(np.ones((128, 128), f))
    return d


_NC_CACHE = {}


def kernel(**inputs):
    if "nc" not in _NC_CACHE:
        _NC_CACHE["nc"] = build_nc()
    nc = _NC_CACHE["nc"]
    shared = _host_shared(inputs)
    in_maps = []
    for c in range(NCORES):
        m = dict(shared)
        m.update(_host_layout(inputs, c))
        in_maps.append(m)
    res = run_bass_kernel_spmd(nc, in_maps, core_ids=list(range(NCORES)))
    outs = [np.asarray(r["out"], np.float32) for r in res.results]
    return np.concatenate(outs, axis=0)[None, :, :]
```

```python
import math
from contextlib import ExitStack
import numpy as np
import concourse.bass as bass
import concourse.mybir as mybir
from concourse.bass_utils import run_bass_kernel_spmd

F32 = mybir.dt.float32
BF16 = mybir.dt.bfloat16
I32 = mybir.dt.int32
AF = mybir.ActivationFunctionType
ALU = mybir.AluOpType
AX = mybir.AxisListType

NCORES = 8
SEQ = 16384
DM = 1024
TOK = SEQ // NCORES
UT = 512
NSEG = 8
NUNITS = NSEG * TOK // UT
FIRST_OWN = NUNITS - TOK // UT
INC = 2568
EPS = 1e-6
SAME_SYNC = True
SAME_DIST = 10 ** 9
SKIP_PP3 = False
PIPE_PREFIX = True
TWO_PI = 2.0 * math.pi


class Sched:
    def __init__(self, nc, es):
        self.nc = nc
        self.eng = {"pe": nc.tensor, "act": nc.scalar, "dve": nc.vector, "pool": nc.gpsimd, "sp": nc.sync}
        self.csem = {e: es.enter_context(nc.semaphore("c_" + e)) for e in ["pe", "act", "dve", "pool"]}
        self.ccnt = {e: 0 for e in self.csem}
        self.NR = 6
        self.dsem = {q: [es.enter_context(nc.semaphore(f"d_{q}{i}")) for i in range(self.NR)]
                     for q in ["sp", "pool"]}
        self.dcnt = {q: 0 for q in self.dsem}
        self.prog = {e: [] for e in self.eng}
        self.lastw = {}
        self.reads = {}
        self.seen = {e: {} for e in self.eng}
        self.out_tokens = []
        self.alias = {}
        self.tick = 0
        self.ps_touch = [0] * 8

    def canon(self, keys):
        out = []
        for k in keys:
            k2 = self.alias.get(k, self.alias.get(k[0], k) if isinstance(k, tuple) else k)
            if k2 not in out:
                out.append(k2)
        return out

    def barrier(self):
        allt = {("c", e): n for e, n in self.ccnt.items() if n > 0}
        for q, n in self.dcnt.items():
            for i in range(max(0, n - self.NR), n):
                allt[("d", q, i % self.NR)] = 16 * (i // self.NR + 1)
        for E in self.eng:
            waits = []
            for key, val in allt.items():
                if key == ("c", E):
                    continue
                if self.seen[E].get(key, 0) >= val:
                    continue
                self.seen[E][key] = val
                waits.append((self._sem(key), val))
            eng = self.eng[E]

            def run(waits=waits, eng=eng):
                for s_, v in waits:
                    eng.wait_ge(s_, v)
            self.prog[E].append(run)

    def _sem(self, key):
        return self.csem[key[1]] if key[0] == "c" else self.dsem[key[1]][key[2]]

    def _emit(self, E, deps, fn, mykey, myval, inc):
        waits = []
        for key, val in deps.items():
            if key == ("c", E) and (E == "pe" or not SAME_SYNC):
                continue
            if key == ("c", E) and self.ccnt[E] - val > SAME_DIST:
                continue
            if self.seen[E].get(key, 0) >= val:
                continue
            self.seen[E][key] = val
            waits.append((self._sem(key), val))
        mysem = self._sem(mykey)
        eng = self.eng[E]

        def run():
            for s, v in waits:
                eng.wait_ge(s, v)
            fn(eng).then_inc(mysem, inc)
        self.prog[E].append(run)

    def _deps(self, reads, writes):
        deps = {}

        def add(d):
            for k, v in d.items():
                if deps.get(k, 0) < v:
                    deps[k] = v
        for k in list(reads) + list(writes):
            if k in self.lastw:
                add(self.lastw[k])
        for k in writes:
            add(self.reads.get(k, {}))
        return deps

    def _commit(self, reads, writes, tok):
        for k in writes:
            self.lastw[k] = dict(tok)
            self.reads[k] = {}
        for k in reads:
            if k in writes:
                continue
            d = self.reads.setdefault(k, {})
            for kk, v in tok.items():
                if d.get(kk, 0) < v:
                    d[kk] = v

    def op(self, E, fn, reads=(), writes=()):
        reads, writes = self.canon(reads), self.canon(writes)
        for k in reads:
            if isinstance(k, tuple) and k[0] == "ps" and k not in writes:
                writes = list(writes) + [k]
        self.tick += 1
        for k in writes:
            if isinstance(k, tuple) and k[0] == "ps":
                self.ps_touch[k[1]] = self.tick
        deps = self._deps(reads, writes)
        self.ccnt[E] += 1
        key, val = ("c", E), self.ccnt[E]
        self._emit(E, deps, fn, key, val, 1)
        self._commit(reads, writes, {key: val})

    def dma(self, q, out, in_, reads=(), writes=(), is_output=False, r=None, w=None):
        reads = r if r is not None else reads
        writes = w if w is not None else writes
        reads, writes = self.canon(reads), self.canon(writes)
        deps = self._deps(reads, writes)
        i = self.dcnt[q]
        self.dcnt[q] += 1
        slot = i % self.NR
        key, val = ("d", q, slot), 16 * (i // self.NR + 1)
        if i >= self.NR:
            if deps.get(key, 0) < val - 16:
                deps[key] = val - 16
        self._emit(q, deps, lambda e: e.dma_start(out=out, in_=in_), key, val, 16)
        self._commit(reads, writes, {key: val})
        if is_output:
            self.out_tokens.append((key, val))

    def finish(self, block):
        fin = {}
        for key, val in self.out_tokens:
            if fin.get(key, 0) < val:
                fin[key] = val
        sp_waits = [(self._sem(k), v) for k, v in fin.items()]

        def sp_final():
            for s, v in sp_waits:
                self.eng["sp"].wait_ge(s, v)
        self.prog["sp"].append(sp_final)

        @block.sync
        def _(e):
            for f in self.prog["sp"]:
                f()

        @block.tensor
        def _(e):
            for f in self.prog["pe"]:
                f()

        @block.scalar
        def _(e):
            for f in self.prog["act"]:
                f()

        @block.vector
        def _(e):
            for f in self.prog["dve"]:
                f()

        @block.gpsimd
        def _(e):
            for f in self.prog["pool"]:
                f()


def build_nc(nunits=NUNITS, dbg_sel=None, only_units=None, stop=None):
    nc = bass.Bass("TRN2", target_bir_lowering=False)
    es = ExitStack()
    with es:
        def din(name, shape):
            return nc.dram_tensor(name, list(shape), F32, kind="ExternalInput").ap()

        xall = din("xall", [NSEG * TOK, DM])
        d_win = din("w_in", [128, 8, INC])
        d_wout = din("w_out", [128, 8, DM])
        d_gluw = din("glu_w", [128, 4, 512])
        d_wff1 = din("w_ff1", [32, 128, 8, 128])
        d_wff2 = din("w_ff2", [32, 128, DM])
        d_cw = din("cw", [128, 8, 4])
        d_cb = din("cb", [128, 8])
        d_gbias = din("gbias", [128, 8])
        d_nwc = din("mnwc", [128, 4])
        d_g1 = din("g1", [128, 8])
        d_g2 = din("g2", [128, 8])
        d_g3 = din("g3", [128, DM])
        d_glub = din("glub", [128, 4])
        d_segm = din("segm", [128, 8])
        d_lr2 = din("lr2", [128, 32])
        d_li2 = din("li2", [128, 32])
        d_ld2 = din("ld2", [128, 32])
        d_lr1 = din("lr1", [128, 256])
        d_li1 = din("li1", [128, 256])
        d_ld1 = din("ld1", [128, 256])
        d_AB = din("AB", [128, 512])
        d_ABs = din("ABs", [128, 512])
        d_AC = din("AC", [128, 512])
        d_ACs = din("ACs", [128, 512])
        d_bT1r = din("bT1r", [128, 256])
        d_bT1i = din("bT1i", [128, 256])
        d_dpad = din("dpad", [128, 4, 16])
        d_cst = din("cst", [128, 384])
        d_id = din("ident", [128, 128])
        d_tri = din("tri", [128, 128])
        out = nc.dram_tensor("out", [TOK, DM], F32, kind="ExternalOutput").ap()
        if dbg_sel is not None:
            dbg = nc.dram_tensor("dbg", [128, 4096], F32, kind="ExternalOutput").ap()
            dbg2 = nc.dram_tensor("dbg2", [128, 32768], F32, kind="ExternalOutput").ap()
        dmap = {}
        dpos = [0]
        nc._dbg_map = dmap
        wff1_s = nc.dram_tensor("wff1_s", [32, 128, 8, 128], BF16).ap()
        wff2_s = nc.dram_tensor("wff2_s", [32, 128, DM], BF16).ap()
        win_s = nc.dram_tensor("win_s", [128, 8, INC], BF16).ap()

        S = Sched(nc, es)

        def sbx(stack, name, shape, dt=F32):
            return stack.enter_context(nc.sbuf_tensor("s_" + name, list(shape), dt))

        def sb(name, shape, dt=F32):
            return sbx(es, name, shape, dt)

        wvg = sb("wvg", [128, 8, 520], BF16)
        cw = sb("cw", [128, 8, 4]); cb = sb("cb", [128, 8]); gbias = sb("gbias", [128, 8])
        mnwc = sb("mnwc", [128, 4]); g1 = sb("g1", [128, 8]); g2 = sb("g2", [128, 8]); g3 = sb("g3", [128, DM])
        glub = sb("glub", [128, 4]); segm = sb("segm", [128, 8])
        cst = sb("cst", [128, 384])
        ident_b = sb("ident_b", [128, 128], BF16)
        tri = sb("tri", [128, 128]); ones_f = sb("ones_f", [128, 128]); trimask = sb("trimask", [128, 128], BF16)
        ones64 = ones_f[:, 0:64]
        rot_b = sb("rot_b", [128, 128], BF16)
        mhalf = sb("mhalf", [128, 4])
        MV9 = cst[:, 0:9]
        MVE = cst[:, 16:80]
        MVF = cst[:, 80:144]
        MVS = cst[:, 144:152]
        SGNA = cst[:, 152:153]
        SGNB = cst[:, 153:154]
        PAR = cst[:, 154:156]
        MASKG = cst[:, 156:164]
        C512 = cst[:, 164:165]
        C1 = cst[:, 165:166]
        rot = cst[:, 256:384]

        ps = [es.enter_context(nc.psum_tensor(f"ps{i}", [128, 512], F32)) for i in range(8)]
        psc = [0]

        ps_pin = set()

        def nextps(pin=False):
            cand = [b for b in range(8) if b not in ps_pin]
            assert cand, "all PSUM banks pinned"
            i = min(cand, key=lambda b: (S.ps_touch[b], b))
            S.tick += 1
            S.ps_touch[i] = S.tick
            if pin:
                ps_pin.add(i)
            return i

        def unpin(banks):
            for b in banks:
                ps_pin.discard(b)

        def E(eng_, method, *args, r=(), w=(), **kw):
            S.op(eng_, lambda e: getattr(e, method)(*args, **kw), reads=r, writes=w)

        def load(dst, src, key, q="sp"):
            S.dma(q, dst, src, w=[key])

        dtmp = sb("dtmp", [128, 512]) if dbg_sel == "unit" else None

        def dump(name, ap, keys, npart=128):
            if dbg_sel != "unit":
                return
            n = ap.shape[1]
            E("dve", "tensor_copy", out=dtmp[0:npart, 0:n], in_=ap, r=list(keys), w=["dtmp"])
            S.dma("sp", dbg2[0:npart, dpos[0]:dpos[0] + n], dtmp[0:npart, 0:n], reads=["dtmp"], is_output=True)
            dmap[name] = (dpos[0], n, npart)
            dpos[0] += n

        load(wvg[:, :, 0:512], d_win[:, :, 1024:1536], "wvg", q="pool")
        load(wvg[:, :, 512:520], d_win[:, :, 2048:2056], "wvg", q="pool")
        load(ident_b[:], d_id, "ident_b", q="pool")
        load(trimask[:], d_tri, "trimask", q="pool")
        for t_, d_, k_ in [(cw, d_cw, "cw"), (cb, d_cb, "cb"), (gbias, d_gbias, "gbias"), (mnwc, d_nwc, "mnwc"),
                           (g1, d_g1, "g1"), (g2, d_g2, "g2"), (g3, d_g3, "g3"), (glub, d_glub, "glub"),
                           (segm, d_segm, "segm"), (cst, d_cst, "cst"), (tri, d_tri, "tri")]:
            load(t_[:], d_, k_)
        E("dve", "memset", ones_f[:], 1.0, w=["ones_f"])
        E("dve", "tensor_copy", out=rot_b[:], in_=cst[:, 256:384], r=["cst"], w=["rot_b"])
        E("pool", "memset", mhalf[:], -0.5, w=["mhalf"])
        for kc in range(8):
            S.dma("pool", win_s[:, kc, :], d_win[:, kc, :], w=["win_s"])
        for fc in range(32):
            S.dma("pool", wff1_s[fc], d_wff1[fc], w=[("wff1_s", fc)])
            S.dma("pool", wff2_s[fc], d_wff2[fc], w=[("wff2_s", fc)])

        def tt(e_, out_, a, b, op, r, w):
            E(e_, "tensor_tensor", out=out_, in0=a, in1=b, op=op, r=r, w=w)

        def ts(e_, out_, a, s1, s2, op0, op1, r, w):
            E(e_, "tensor_scalar", out=out_, in0=a, scalar1=s1, scalar2=s2, op0=op0, op1=op1, r=r, w=w)

        def stt(e_, out_, a, sc, b, op0, op1, r, w):
            E(e_, "scalar_tensor_tensor", out=out_, in0=a, scalar=sc, in1=b, op0=op0, op1=op1, r=r, w=w)

        def act(out_, in_, func, r, w, bias=None, scale=None):
            kw = {}
            if bias is not None:
                kw["bias"] = bias
            if scale is not None:
                kw["scale"] = scale
            E("act", "activation", out=out_, in_=in_, func=func, **kw, r=r, w=w)

        Er = sb("Er", [128, 32, 64]); Ei = sb("Ei", [128, 32, 64])
        PWB = sb("PWB", [128, 4, 2, 8, 128], BF16)
        Fr_d = nc.dram_tensor("Fr_d", [128, 2048], F32).ap(); Fi_d = nc.dram_tensor("Fi_d", [128, 2048], F32).ap()
        CZ_d = nc.dram_tensor("CZ_d", [128, 4608], BF16).ap(); PWT_d = nc.dram_tensor("PWT_d", [128, 8192], BF16).ap()
        L5r = sb("L5r", [128, 1, 32]); L5i = sb("L5i", [128, 1, 32])
        Scar = sb("Scar", [128, 2, 32])
        Cst_f = sb("Cst_f", [128, 4, 129]); Cst_b = sb("Cst_b", [128, 4, 129], BF16)
        hal = sb("hal", [128, 8, 3])

        with ExitStack() as ps_es:
            def sp(name, shape, dt=F32):
                return sbx(ps_es, name, shape, dt)
            Fr = sp("Fr", [128, 32, 64]); Fi = sp("Fi", [128, 32, 64])
            CZ = sp("CZ", [128, 32, 9, 16], BF16)
            PWT = sp("PWT", [128, 4, 2, 8, 128], BF16)
            PWd = 2048
            t_a = sp("t_a", [128, PWd]); t_k = sp("t_k", [128, PWd]); t_s4 = sp("t_s4", [128, PWd]); t_m = sp("t_m", [128, PWd])
            TK = ["tk"]

            def cpow(theta3, rho3, m3, A, B, out_re, out_im, rk, wk):
                n = A * B

                def v(t, dt=None):
                    ap = t[:, 0:n]
                    if dt is not None:
                        ap = ap.bitcast(dt)
                    return ap.rearrange("p (a b) -> p a b", a=A)
                tt("dve", v(t_a), theta3, m3, ALU.mult, rk + ["cst"] + TK, TK)
                ts("dve", v(t_k), v(t_a), 1.0 / TWO_PI, None, ALU.mult, ALU.bypass, TK, TK)
                E("dve", "tensor_copy", out=v(t_m, I32), in_=v(t_k), r=TK, w=TK)
                E("dve", "tensor_copy", out=v(t_k), in_=v(t_m, I32), r=TK, w=TK)
                stt("dve", v(t_a), v(t_k), -TWO_PI, v(t_a), ALU.mult, ALU.add, TK, TK)
                act(v(t_k), v(t_a), AF.Sin, TK, TK, scale=0.5)
                act(v(t_s4), v(t_a), AF.Sin, TK, TK, scale=0.25)
                tt("dve", v(t_s4), v(t_s4), v(t_s4), ALU.mult, TK, TK)
                ts("dve", v(t_s4), v(t_s4), -2.0, 1.0, ALU.mult, ALU.add, TK, TK)
                stt("dve", v(t_a), v(t_k), 2.0, v(t_s4), ALU.mult, ALU.mult, TK, TK)
                tt("dve", v(t_s4), v(t_k), v(t_k), ALU.mult, TK, TK)
                ts("dve", v(t_s4), v(t_s4), -2.0, 1.0, ALU.mult, ALU.add, TK, TK)
                tt("dve", v(t_m), rho3, m3, ALU.mult, rk + ["cst"] + TK, TK)
                act(v(t_m), v(t_m), AF.Exp, TK, TK)
                tt("dve", out_re, v(t_m), v(t_s4), ALU.mult, TK, wk)
                tt("dve", out_im, v(t_m), v(t_a), ALU.mult, TK, wk)

            def exp_acc(x, shape, key, tmpn):
                y = sp(tmpn + "_y", shape); p = sp(tmpn + "_p", shape)
                T = [tmpn]
                ts("dve", y[:], x, 0.125, None, ALU.mult, ALU.bypass, [key], T)
                E("dve", "memset", p[:], 1.0, r=T, w=T)
                for k in range(12, 0, -1):
                    tt("dve", p[:], p[:], y[:], ALU.mult, T, T)
                    ts("dve", p[:], p[:], 1.0 / k, 1.0, ALU.mult, ALU.add, T, T)
                for _ in range(3):
                    tt("dve", p[:], p[:], p[:], ALU.mult, T, T)
                E("dve", "tensor_copy", out=x, in_=p[:], r=T + [key], w=[key])

            lr2 = sp("lr2", [128, 32]); li2 = sp("li2", [128, 32]); dt2 = sp("dt2", [128, 32])
            rho2 = sp("rho2", [128, 32]); th2 = sp("th2", [128, 32])
            load(lr2[:], d_lr2, "lr2"); load(li2[:], d_li2, "li2"); load(dt2[:], d_ld2, "dt2")
            exp_acc(dt2[:], [128, 32], "dt2", "ex2")
            tt("dve", rho2[:], lr2[:], dt2[:], ALU.mult, ["lr2", "dt2"], ["rho2"])
            tt("dve", th2[:], li2[:], dt2[:], ALU.mult, ["li2", "dt2"], ["th2"])
            Ld_re = sp("Ld_re", [128, 9, 32]); Ld_im = sp("Ld_im", [128, 9, 32])
            cpow(th2[:].unsqueeze(1).to_broadcast([128, 9, 32]), rho2[:].unsqueeze(1).to_broadcast([128, 9, 32]),
                 MV9.unsqueeze(2).to_broadcast([128, 9, 32]), 9, 32, Ld_re[:], Ld_im[:], ["th2", "rho2"], ["Ld"])

            def kappa(lr, li, lbr, lbi, kr, ki, shape, rk, wk, tmpn):
                a = sp(tmpn + "_a", shape); b = sp(tmpn + "_b", shape); c = sp(tmpn + "_c", shape)
                T = [tmpn]
                ts("dve", a[:], lbr, -1.0, None, ALU.add, ALU.bypass, rk, T)
                tt("dve", b[:], lr, lr, ALU.mult, rk + T, T)
                tt("dve", c[:], li, li, ALU.mult, rk + T, T)
                tt("dve", b[:], b[:], c[:], ALU.add, T, T)
                E("dve", "reciprocal", out=b[:], in_=b[:], r=T, w=T)
                tt("dve", kr, a[:], lr, ALU.mult, rk + T, wk)
                tt("dve", c[:], lbi, li, ALU.mult, rk + T, T)
                tt("dve", kr, kr, c[:], ALU.add, wk + T, wk)
                tt("dve", kr, kr, b[:], ALU.mult, wk + T, wk)
                tt("dve", ki, lbi, lr, ALU.mult, rk + T, wk)
                tt("dve", c[:], a[:], li, ALU.mult, rk + T, T)
                tt("dve", ki, ki, c[:], ALU.subtract, wk + T, wk)
                tt("dve", ki, ki, b[:], ALU.mult, wk + T, wk)

            k2r = sp("k2r", [128, 32]); k2i = sp("k2i", [128, 32])
            kappa(lr2[:], li2[:], Ld_re[:, 1, :], Ld_im[:, 1, :], k2r[:], k2i[:], [128, 32], ["lr2", "li2", "Ld"], ["k2"], "kp2")
            AB = sp("AB", [128, 32, 16]); ABs = sp("ABs", [128, 32, 16]); AC = sp("AC", [128, 32, 16]); ACs = sp("ACs", [128, 32, 16])
            load(AB[:].rearrange("p g c -> p (g c)"), d_AB, "AB"); load(ABs[:].rearrange("p g c -> p (g c)"), d_ABs, "ABs")
            load(AC[:].rearrange("p g c -> p (g c)"), d_AC, "AC"); load(ACs[:].rearrange("p g c -> p (g c)"), d_ACs, "ACs")
            k2is = sp("k2is", [128, 32])
            ts("dve", k2is[:], k2i[:], SGNA, None, ALU.mult, ALU.bypass, ["k2", "cst"], ["k2is"])
            tmpB = sp("tmpB", [128, 32, 16]); tmpB2 = sp("tmpB2", [128, 32, 16])
            BbS = sp("BbS", [128, 32, 16], BF16)
            tt("dve", tmpB[:], AB[:], k2r[:].unsqueeze(2).to_broadcast([128, 32, 16]), ALU.mult, ["AB", "k2"], ["tmpB"])
            tt("dve", tmpB2[:], ABs[:], k2is[:].unsqueeze(2).to_broadcast([128, 32, 16]), ALU.mult, ["ABs", "k2is"], ["tmpB2"])
            tt("dve", BbS[:], tmpB[:], tmpB2[:], ALU.add, ["tmpB", "tmpB2"], ["BbS"])
            P1 = sp("P1", [128, 9, 32]); P2 = sp("P2", [128, 9, 32])
            ts("dve", P1[:], Ld_re[:], SGNB, None, ALU.mult, ALU.bypass, ["Ld", "cst"], ["P1"])
            ts("dve", P2[:], Ld_im[:], -1.0, None, ALU.mult, ALU.bypass, ["Ld"], ["P2"])
            for d in range(9):
                tt("dve", tmpB[:], AC[:], P1[:, d, :].unsqueeze(2).to_broadcast([128, 32, 16]), ALU.mult, ["AC", "P1", "tmpB"], ["tmpB"])
                tt("dve", tmpB2[:], ACs[:], P2[:, d, :].unsqueeze(2).to_broadcast([128, 32, 16]), ALU.mult, ["ACs", "P2", "tmpB2"], ["tmpB2"])
                tt("dve", CZ[:, :, d, :], tmpB[:], tmpB2[:], ALU.add, ["tmpB", "tmpB2"], ["CZ"])
            thb = th2[:].unsqueeze(2).to_broadcast([128, 32, 64]); rhb = rho2[:].unsqueeze(2).to_broadcast([128, 32, 64])
            cpow(thb, rhb, MVE.unsqueeze(1).to_broadcast([128, 32, 64]), 32, 64, Er[:], Ei[:], ["th2", "rho2"], ["Etab"])
            m2 = t_a[:, 0:2048].rearrange("p (a b) -> p a b", a=32)
            m3 = t_k[:, 0:2048].rearrange("p (a b) -> p a b", a=32)
            tt("dve", m2, Er[:], Er[:], ALU.mult, ["Etab"] + TK, TK)
            tt("dve", m3, Ei[:], Ei[:], ALU.mult, ["Etab"] + TK, TK)
            tt("dve", m2, m2, m3, ALU.add, TK, TK)
            E("dve", "reciprocal", out=m2, in_=m2, r=TK, w=TK)
            tt("dve", Fr[:], Er[:], m2, ALU.mult, ["Etab"] + TK, ["Ftab"])
            stt("dve", Fi[:], Ei[:], -1.0, m2, ALU.mult, ALU.mult, ["Etab"] + TK, ["Ftab"])
            cpow(th2[:].unsqueeze(1), rho2[:].unsqueeze(1), C512.unsqueeze(2).to_broadcast([128, 1, 32]), 1, 32,
                 L5r[:], L5i[:], ["th2", "rho2"], ["L5"])
            lr1 = sp("lr1", [128, 256]); li1 = sp("li1", [128, 256]); dt1 = sp("dt1", [128, 256])
            rho1 = sp("rho1", [128, 256]); th1 = sp("th1", [128, 256])
            load(lr1[:], d_lr1, "lr1"); load(li1[:], d_li1, "li1"); load(dt1[:], d_ld1, "dt1")
            exp_acc(dt1[:], [128, 256], "dt1", "ex1")
            tt("dve", rho1[:], lr1[:], dt1[:], ALU.mult, ["lr1", "dt1"], ["rho1"])
            tt("dve", th1[:], li1[:], dt1[:], ALU.mult, ["li1", "dt1"], ["th1"])
            lb1r = sp("lb1r", [128, 1, 256]); lb1i = sp("lb1i", [128, 1, 256])
            cpow(th1[:].unsqueeze(1), rho1[:].unsqueeze(1), C1.unsqueeze(2).to_broadcast([128, 1, 256]), 1, 256,
                 lb1r[:], lb1i[:], ["th1", "rho1"], ["lb1"])
            k1r = sp("k1r", [128, 256]); k1i = sp("k1i", [128, 256])
            kappa(lr1[:], li1[:], lb1r[:, 0, :], lb1i[:, 0, :], k1r[:], k1i[:], [128, 256], ["lr1", "li1", "lb1"], ["k1"], "kp1")
            Fs_r = sp("Fs_r", [128, 8, 256]); Fs_i = sp("Fs_i", [128, 8, 256])
            cpow(th1[:].unsqueeze(1).to_broadcast([128, 8, 256]), rho1[:].unsqueeze(1).to_broadcast([128, 8, 256]),
                 MVS.unsqueeze(2).to_broadcast([128, 8, 256]), 8, 256, Fs_r[:], Fs_i[:], ["th1", "rho1"], ["Fs"])
            bT1r = sp("bT1r", [128, 256]); bT1i = sp("bT1i", [128, 256])
            load(bT1r[:], d_bT1r, "bT1r"); load(bT1i[:], d_bT1i, "bT1i")
            kbr = sp("kbr", [128, 256]); kbi = sp("kbi", [128, 256]); tq = sp("tq", [128, 256])
            tt("dve", kbr[:], k1r[:], bT1r[:], ALU.mult, ["k1", "bT1r"], ["kbr"])
            tt("dve", tq[:], k1i[:], bT1i[:], ALU.mult, ["k1", "bT1i"], ["tq"])
            tt("dve", kbr[:], kbr[:], tq[:], ALU.subtract, ["kbr", "tq"], ["kbr"])
            tt("dve", kbi[:], k1r[:], bT1i[:], ALU.mult, ["k1", "bT1i"], ["kbi"])
            tt("dve", tq[:], k1i[:], bT1r[:], ALU.mult, ["k1", "bT1r", "kbr"], ["tq"])
            tt("dve", kbi[:], kbi[:], tq[:], ALU.add, ["kbi", "tq"], ["kbi"])
            Bn_r = t_a[:, 0:2048].rearrange("p (a b) -> p a b", a=8)
            Bn_i = t_k[:, 0:2048].rearrange("p (a b) -> p a b", a=8)
            t8 = t_s4[:, 0:2048].rearrange("p (a b) -> p a b", a=8)
            kbrb = kbr[:].unsqueeze(1).to_broadcast([128, 8, 256]); kbib = kbi[:].unsqueeze(1).to_broadcast([128, 8, 256])
            tt("dve", Bn_r, Fs_r[:], kbrb, ALU.mult, ["Fs", "kbr"] + TK, TK)
            tt("dve", t8, Fs_i[:], kbib, ALU.mult, ["Fs", "kbi"] + TK, TK)
            tt("dve", Bn_r, Bn_r, t8, ALU.subtract, TK, TK)
            tt("dve", Bn_i, Fs_r[:], kbib, ALU.mult, ["Fs", "kbi"] + TK, TK)
            tt("dve", t8, Fs_i[:], kbrb, ALU.mult, ["Fs", "kbr"] + TK, TK)
            tt("dve", Bn_i, Bn_i, t8, ALU.add, TK, TK)
            for j in range(4):
                for e_ in range(2):
                    pe_ = PAR[:, e_:e_ + 1]
                    ts("dve", PWB[:, j, e_, :, 0:64], Bn_r[:, :, j * 64:(j + 1) * 64], pe_, None, ALU.mult, ALU.bypass, TK + ["cst"], ["PWB"])
                    ts("dve", PWB[:, j, e_, :, 64:128], Bn_i[:, :, j * 64:(j + 1) * 64], pe_, None, ALU.mult, ALU.bypass, TK + ["cst"], ["PWB"])
            Kt = sp("Kt", [128, 4, 2, 128])
            E("dve", "memset", Kt[:], 0.0, w=["Kt"])
            dpad = sp("dpad", [128, 4, 16])
            load(dpad[:], d_dpad, "dpad")
            for g in range(32):
                j, g8 = g // 8, g % 8
                e_ = g8 % 2
                pi = nextps()
                E("pe", "matmul", ps[pi][:, 0:128], BbS[:, 8 * j:8 * j + 8, :].rearrange("p g c -> p (g c)"),
                                                             CZ[:, g, 0:8, :].rearrange("p d c -> p (d c)"), start=True, stop=True, r=["BbS", "CZ"], w=[("ps", pi)])
                stt("dve", Kt[:, j, e_, :], ps[pi][:, 0:128], MASKG[:, g8:g8 + 1], Kt[:, j, e_, :], ALU.mult, ALU.add,
                    [("ps", pi), "cst", "Kt"], ["Kt"])
            for j in range(4):
                for e_ in range(2):
                    stt("dve", Kt[:, j, e_, 0:16], dpad[:, j, :], PAR[:, e_:e_ + 1], Kt[:, j, e_, 0:16], ALU.mult, ALU.add,
                        ["dpad", "cst", "Kt"], ["Kt"])
            E("pool", "memset", PWT[:], 0.0, w=["PWT"])
            for j in range(4):
                for sg in range(8):
                    L = (8 - sg) * 16
                    E("dve", "tensor_copy", out=PWT[:, j, :, sg, sg * 16:128], in_=Kt[:, j, :, 0:L], r=["Kt"], w=["PWT"])
            if dbg_sel == "prep":
                S.dma("sp", dbg[:, 0:288], Ld_re[:].rearrange("p a b -> p (a b)"), r=["Ld"], is_output=True)
                S.dma("sp", dbg[:, 288:576], Ld_im[:].rearrange("p a b -> p (a b)"), r=["Ld"], is_output=True)
                S.dma("sp", dbg[:, 576:608], k2r[:], reads=["k2"], is_output=True)
                S.dma("sp", dbg[:, 608:640], k2i[:], reads=["k2"], is_output=True)
                S.dma("sp", dbg[:, 1024:2048], Kt[:].rearrange("p a b c -> p (a b c)"), r=["Kt"], is_output=True)
                S.dma("sp", dbg[:, 2048:4096], Er[:].rearrange("p a b -> p (a b)"), r=["Etab"], is_output=True)
            S.dma("sp", Fr_d, Fr[:].rearrange("p g n -> p (g n)"), reads=["Ftab"], writes=["Fr_d"])
            S.dma("sp", Fi_d, Fi[:].rearrange("p g n -> p (g n)"), reads=["Ftab"], writes=["Fi_d"])
            S.dma("sp", CZ_d, CZ[:].rearrange("p g d c -> p (g d c)"), reads=["CZ"], writes=["CZ_d"])
            S.dma("sp", PWT_d, PWT[:].rearrange("p j e s m -> p (j e s m)"), reads=["PWT"], writes=["PWT_d"])
            S.barrier()

        E("dve", "memset", Cst_f[:], 0.0, w=["Cst_f"])
        E("dve", "memset", Cst_b[:], 0.0, w=["Cst_b"])
        E("dve", "memset", Scar[:], 0.0, w=["Scar"])
        E("dve", "memset", hal[:], 0.0, w=["hal"])

        unit_list = list(only_units) if only_units is not None else list(range(NUNITS - nunits, NUNITS))
        pre_units = [u for u in unit_list if u < FIRST_OWN]
        own_units = [u for u in unit_list if u >= FIRST_OWN]
        if pre_units and PIPE_PREFIX:
            with ExitStack() as px:
                def sq(name, shape, dt=F32):
                    return sbx(px, name, shape, dt)
                NXR = 8
                xr = [sq(f"xr{i}", [128, DM]) for i in range(NXR)]
                xnr = [sq(f"xnr{i}", [128, DM], BF16) for i in range(4)]
                bst_p = sq("bst_p", [128, 2, 6]); bag_p = sq("bag_p", [128, 2])
                rs_p = [sq(f"rs_p{b}", [128, 8]) for b in range(2)]
                hTp = [sq(f"hTp{b}", [128, 8, UT], BF16) for b in range(2)]
                preb = [sq(f"preb{b}", [128, UT + 3], BF16) for b in range(2)]
                kTp = [sq(f"kTp{b}", [128, 4, UT], BF16) for b in range(2)]
                uTp = [sq(f"uTp{b}", [128, 4, UT], BF16) for b in range(2)]
                vtp = [sq(f"vtp{b}", [128, 4, 4, 129], BF16) for b in range(2)]
                gatp = [sq(f"gatp{b}", [128, 4, 8]) for b in range(2)]
                splp = [sq(f"splp{b}", [128, 4, 4]) for b in range(2)]
                gap = [sq(f"gap{b}", [128, 4, 4]) for b in range(2)]
                gkLp = [sq(f"gkLp{b}", [128, 4, 4]) for b in range(2)]
                sufp = [sq(f"sufp{b}", [128, 4, 4]) for b in range(2)]
                decp = [sq(f"decp{b}", [128, 4]) for b in range(2)]
                khp = [sq(f"khp{b}", [128, 4, 4, 128], BF16) for b in range(2)]
                Wsp = [sq(f"Wsp{i}", [128, 256], BF16) for i in range(4)]
                ZtP = [sq(f"ZtP{b}", [128, 32, 64]) for b in range(2)]
                ztp = [sq(f"ztp{i}", [128, 256]) for i in range(2)]
                sumz = sq("sumz", [128, 32]); send = sq("send", [128, 32]); cr2 = sq("cr2", [128, 2, 32])
                wku = sq("wku", [128, 8, 1024], BF16)
                Dg = sq("Dg", [128, 4, 4, 128], BF16)
                halb = sq("halb", [128, 4, 3], BF16)
                for kc in range(8):
                    S.dma("sp", wku[:, kc, 0:512], win_s[:, kc, 512:1024], reads=["win_s"], writes=["wku"])
                    S.dma("sp", wku[:, kc, 512:1024], win_s[:, kc, 2056:2568], reads=["win_s"], writes=["wku"])
                E("dve", "memset", halb[:], 0.0, w=["halb"])
                for b in range(2):
                    E("pool", "memset", vtp[b][:], 1.0, w=[("vtp", b)])
                    E("pool", "memset", sufp[b][:], 0.0, w=[("sufp", b)])
                for c in range(4):
                    for jj in range(4):
                        ts("dve", Dg[:, c, jj, :], ident_b[:], cw[:, 4 + c, jj:jj + 1], None, ALU.mult, ALU.bypass,
                           ["ident_b", "cw"], ["Dg"])
                Er5 = Er[:].rearrange("p (j q e) n -> p j q e n", j=4, q=4)
                Ei5 = Ei[:].rearrange("p (j q e) n -> p j q e n", j=4, q=4)

                def stageA1(u):
                    b = u % 2
                    tok0 = u * UT
                    for t in range(4):
                        xi = (4 * u + t) % NXR
                        X = xr[xi]
                        S.dma("sp", X[:], xall[tok0 + t * 128: tok0 + (t + 1) * 128, :], writes=[("xr", xi)])
                        E("dve", "bn_stats", out=bst_p[:, 0, :], in_=X[:, 0:512], r=[("xr", xi)], w=["bst_p"])
                        E("dve", "bn_stats", out=bst_p[:, 1, :], in_=X[:, 512:1024], r=[("xr", xi)], w=["bst_p"])
                        E("dve", "bn_aggr", out=bag_p[:], in_=bst_p[:].rearrange("p a b -> p (a b)"), r=["bst_p"], w=["bag_p"])
                        stt("dve", rs_p[b][:, t:t + 1], bag_p[:, 0:1], bag_p[:, 0:1], bag_p[:, 1:2], ALU.mult, ALU.add,
                            ["bag_p"], [("rs_p", b)])
                    ts("dve", rs_p[b][:, 4:8], rs_p[b][:, 0:4], EPS, None, ALU.add, ALU.bypass, [("rs_p", b)], [("rs_p", b)])

                def stageA1b(u):
                    b = u % 2
                    act(rs_p[b][:, 4:8], rs_p[b][:, 4:8], AF.Sqrt, [("rs_p", b)], [("rs_p", b)])
                    E("dve", "reciprocal", out=rs_p[b][:, 4:8], in_=rs_p[b][:, 4:8], r=[("rs_p", b)], w=[("rs_p", b)])

                def stageA2(u):
                    b = u % 2
                    for t in range(4):
                        xi = (4 * u + t) % NXR
                        act(xnr[t][:], xr[xi][:], AF.Copy, [("xr", xi), ("rs_p", b)], [("xnr", t)], scale=rs_p[b][:, 4 + t:5 + t])
                    for tp_ in range(2):
                        pbs = [nextps(), nextps()]
                        for t in (2 * tp_, 2 * tp_ + 1):
                            for c in range(8):
                                pi = pbs[c // 4]
                                pb = ps[pi][:].bitcast(BF16)
                                col = ((c % 4) * 2 + (t % 2)) * 128
                                E("pe", "transpose", pb[:, col:col + 128], xnr[t][:, c * 128:(c + 1) * 128], ident_b[:],
                                  r=[("xnr", t), "ident_b"], w=[("ps", pi)])
                        t = 2 * tp_ + 1
                        for half in range(2):
                            pi = pbs[half]
                            pb = ps[pi][:].bitcast(BF16)
                            for cc in range(4):
                                c = half * 4 + cc
                                dst = hTp[b][:, c, (t - 1) * 128:(t + 1) * 128]
                                src = pb[:, cc * 256:(cc + 1) * 256]
                                if half == 0:
                                    act(dst, src, AF.Copy, [("ps", pi), "g1"], [("hTp", b, c)], scale=g1[:, c:c + 1])
                                else:
                                    ts("dve", dst, src, g1[:, c:c + 1], None, ALU.mult, ALU.bypass, [("ps", pi), "g1"], [("hTp", b, c)])

                def stageB(u):
                    b = u % 2
                    seg = u // 4
                    hT_ = hTp[b]
                    def kproj(c):
                        pi = nextps()
                        for kc in range(8):
                            E("pe", "matmul", ps[pi][:], wku[:, kc, c * 128:(c + 1) * 128], hT_[:, kc, :], start=(kc == 0), stop=(kc == 7),
                              r=["wku", ("hTp", b, kc)], w=[("ps", pi)])
                        pr = preb[c % 2]
                        kp = ("preb", c % 2)
                        E("pool", "tensor_copy", out=pr[:, 0:3], in_=halb[:, c, :], r=["halb"], w=[kp])
                        act(pr[:, 3:UT + 3], ps[pi][:], AF.Copy, [("ps", pi)], [kp])
                        E("pool", "tensor_copy", out=halb[:, c, :], in_=pr[:, UT:UT + 3], r=[kp], w=["halb"])

                    def kconv(c):
                        pr = preb[c % 2]
                        kp = ("preb", c % 2)
                        pj = nextps()
                        for jj in range(4):
                            E("pe", "matmul", ps[pj][:], Dg[:, c, jj, :], pr[:, jj:jj + UT], start=(jj == 0), stop=(jj == 3),
                              r=["Dg", kp], w=[("ps", pj)])
                        act(kTp[b][:, c, :], ps[pj][:], AF.Silu, [("ps", pj), "cb"], [("kTp", b)], bias=cb[:, 4 + c:5 + c])
                    for c in range(4):
                        kproj(c)
                        if c > 0:
                            kconv(c - 1)
                    kconv(3)
                    for c in range(4):
                        pi = nextps()
                        for kc in range(8):
                            E("pe", "matmul", ps[pi][:], wku[:, kc, 512 + c * 128:512 + (c + 1) * 128], hT_[:, kc, :],
                              start=(kc == 0), stop=(kc == 7), r=["wku", ("hTp", b, kc)], w=[("ps", pi)])
                        act(uTp[b][:, c, :], ps[pi][:], AF.Copy, [("ps", pi)], [("uTp", b)])
                    if u == FIRST_OWN - 1:
                        for c in range(4):
                            bi = wsl[0] % 2
                            wsl[0] += 1
                            wb = w1b_early[bi]
                            S.dma("sp", wb[:], win_s[:, :, c * 128:(c + 1) * 128], reads=["win_s"], writes=[("w1be", bi)])
                            pi = nextps()
                            for kc in range(8):
                                E("pe", "matmul", ps[pi][:], wb[:, kc, :], hT_[:, kc, :], start=(kc == 0), stop=(kc == 7),
                                  r=[("w1be", bi), ("hTp", b, kc)], w=[("ps", pi)])
                            E("dve", "tensor_copy", out=hal[:, c, :], in_=ps[pi][:, UT - 3:UT], r=[("ps", pi)], w=["hal"])
                    pg = nextps()
                    for t in range(4):
                        for kc in range(8):
                            E("pe", "matmul", ps[pg][:, t * 8:(t + 1) * 8], hT_[:, kc, t * 128:(t + 1) * 128], wvg[:, kc, 512:520],
                              start=(kc == 0), stop=(kc == 7), r=["wvg", ("hTp", b, kc)], w=[("ps", pg)])
                    G_ = [("gatp", b)]
                    tt("dve", gatp[b][:], ps[pg][:, 0:32].rearrange("p (t g) -> p t g", t=4),
                       gbias[:].unsqueeze(1).to_broadcast([128, 4, 8]), ALU.add, [("ps", pg), "gbias"], G_)
                    act(splp[b][:], gatp[b][:, :, 4:8], AF.Exp, G_, [("splp", b)], scale=-1.0)
                    act(splp[b][:], splp[b][:], AF.Ln, [("splp", b)], [("splp", b)], bias=1.0)
                    for t in range(4):
                        pi = nextps()
                        for kc in range(8):
                            E("pe", "matmul", ps[pi][:], hT_[:, kc, t * 128:(t + 1) * 128], wvg[:, kc, 0:512], start=(kc == 0), stop=(kc == 7),
                              r=["wvg", ("hTp", b, kc)], w=[("ps", pi)])
                        act(vtp[b][:, t, :, 0:128], ps[pi][:].rearrange("p (h d) -> p h d", h=4), AF.Copy, [("ps", pi)], [("vtp", b)])
                    pq = nextps()
                    sp2 = splp[b][:].rearrange("p t h -> p (t h)")
                    E("pe", "matmul", ps[pq][:, 0:16], tri[:], sp2, start=True, stop=True, r=["tri", ("splp", b)], w=[("ps", pq)])
                    E("pe", "matmul", ps[pq][:, 16:32], ones_f[:], sp2, start=True, stop=True, r=["ones_f", ("splp", b)], w=[("ps", pq)])
                    cums = ps[pq][:, 0:16].rearrange("p (t h) -> p t h", t=4)
                    tot = ps[pq][:, 16:32].rearrange("p (t h) -> p t h", t=4)
                    E("dve", "tensor_copy", out=sufp[b][:, 2, :], in_=tot[:, 3, :], r=[("ps", pq)], w=[("sufp", b)])
                    tt("dve", sufp[b][:, 1, :], sufp[b][:, 2, :], tot[:, 2, :], ALU.add, [("ps", pq), ("sufp", b)], [("sufp", b)])
                    tt("dve", sufp[b][:, 0, :], sufp[b][:, 1, :], tot[:, 1, :], ALU.add, [("ps", pq), ("sufp", b)], [("sufp", b)])
                    tt("dve", gap[b][:], gatp[b][:, :, 0:4], cums, ALU.add, G_ + [("ps", pq)], [("gap", b)])
                    tt("dve", gap[b][:], gap[b][:], tot, ALU.subtract, [("gap", b), ("ps", pq)], [("gap", b)])
                    tt("dve", gap[b][:], gap[b][:], sufp[b][:], ALU.subtract, [("gap", b), ("sufp", b)], [("gap", b)])
                    act(gkLp[b][:], gap[b][:], AF.Exp, [("gap", b)], [("gkLp", b)])
                    ts("dve", gkLp[b][:], gkLp[b][:], segm[:, seg:seg + 1], None, ALU.mult, ALU.bypass, [("gkLp", b), "segm"], [("gkLp", b)])
                    tt("dve", decp[b][:], sufp[b][:, 0, :], tot[:, 0, :], ALU.add, [("sufp", b), ("ps", pq)], [("decp", b)])
                    act(decp[b][:], decp[b][:], AF.Exp, [("decp", b)], [("decp", b)], scale=-1.0)

                carry_pending = [False]

                def stageC(u):
                    b = u % 2
                    if carry_pending[0]:
                        carry_finish()
                        carry_pending[0] = False
                    for t in range(4):
                        pi = nextps()
                        pb = ps[pi][:].bitcast(BF16)
                        for h in range(4):
                            E("pe", "transpose", pb[:, h * 128:(h + 1) * 128], kTp[b][:, h, t * 128:(t + 1) * 128], ident_b[:],
                              r=[("kTp", b), "ident_b"], w=[("ps", pi)])
                        tt("dve", khp[b][:, t, :, :], pb[:, 0:512].rearrange("p (h d) -> p h d", h=4),
                           gkLp[b][:, t, :].unsqueeze(2).to_broadcast([128, 4, 128]), ALU.mult, [("ps", pi), ("gkLp", b)], [("khp", b)])
                    Zt5 = ZtP[b][:].rearrange("p (j q e) n -> p j q e n", j=4, q=4)
                    S5C = True
                    bks = {}

                    def s5_mm(bb):
                        bk = [nextps(pin=True) for _ in range(4)]
                        bks[bb] = bk
                        for jj in range(2):
                            j = 2 * bb + jj
                            for e_ in range(2):
                                for sg in range(8):
                                    for pp in range(4):
                                        pi = bk[pp]
                                        pv = ps[pi][:].rearrange("p (w j e n) -> p w j e n", w=2, j=2, e=2)
                                        E("pe", "matmul", pv[:, 0, jj, e_, :], PWB[32 * pp:32 * pp + 32, j, e_, sg, :],
                                          uTp[b][32 * pp:32 * pp + 32, j, sg:UT:8], start=(sg == 0), stop=(sg == 7),
                                          tile_position=(32 * pp, 0), r=["PWB", ("uTp", b)], w=[("ps", pi)])

                    def s5_post(bb):
                        bk = bks[bb]
                        for pp in range(4):
                            pi = bk[pp]
                            act(Wsp[pp][:], ps[pi][:, 0:256], AF.Copy, [("ps", pi)], [("Wsp", pp)])
                            E("pe", "matmul", ps[pi][:, 256:512], rot_b[:], Wsp[pp][:], start=True, stop=True,
                              r=["rot_b", ("Wsp", pp)], w=[("ps", pi)])
                            zi = pp % 2
                            zv = ztp[zi][:].rearrange("p (j e n) -> p j e n", j=2, e=2)
                            w1v = Wsp[pp][:].rearrange("p (j e n) -> p j e n", j=2, e=2)
                            w2v = ps[pi][:, 256:512].rearrange("p (j e n) -> p j e n", j=2, e=2)
                            zo = Zt5[:, 2 * bb:2 * bb + 2, pp, :, :]
                            tt("dve", zo, Er5[:, 2 * bb:2 * bb + 2, pp, :, :], w1v, ALU.mult, ["Etab", ("Wsp", pp)], [("ZtP", b)])
                            tt("dve", zv, Ei5[:, 2 * bb:2 * bb + 2, pp, :, :], w2v, ALU.mult, ["Etab", ("ps", pi)], [("ztp", zi)])
                            tt("dve", zo, zo, zv, ALU.add, [("ZtP", b), ("ztp", zi)], [("ZtP", b)])
                        unpin(bk)

                    s5_mm(0)
                    for h in range(4):
                        pi = nextps()
                        for t in range(4):
                            E("pe", "matmul", ps[pi][:, 0:129], khp[b][:, t, h, :], vtp[b][:, t, h, :], start=(t == 0), stop=(t == 3),
                              r=[("khp", b), ("vtp", b)], w=[("ps", pi)])
                        stt("dve", Cst_f[:, h, :], Cst_f[:, h, :], decp[b][:, h:h + 1], ps[pi][:, 0:129], ALU.mult, ALU.add,
                            ["Cst_f", ("decp", b), ("ps", pi)], ["Cst_f"])
                    s5_post(0)
                    s5_mm(1)
                    s5_post(1)
                    E("dve", "reduce_sum", out=sumz[:], in_=ZtP[b][:], axis=AX.X, r=[("ZtP", b)], w=["sumz"])
                    tt("dve", send[:], Scar[:, 0, :], sumz[:], ALU.add, ["Scar", "sumz"], ["send"])
                    carry_pending[0] = True

                def carry_finish():
                    pi = nextps()
                    E("pe", "matmul", ps[pi][:, 0:32], rot, send[:], start=True, stop=True, r=["cst", "send"], w=[("ps", pi)])
                    tt("dve", cr2[:, 0, :], L5r[:, 0, :], send[:], ALU.mult, ["L5", "send"], ["cr2"])
                    tt("dve", cr2[:, 1, :], L5i[:, 0, :], ps[pi][:, 0:32], ALU.mult, ["L5", ("ps", pi)], ["cr2"])
                    tt("dve", Scar[:, 0, :], cr2[:, 0, :], cr2[:, 1, :], ALU.add, ["cr2"], ["Scar"])

                w1b_early = [sq(f"w1be{i}", [128, 8, 128], BF16) for i in range(2)]
                wsl = [0]
                n_ = len(pre_units)
                stageA1(pre_units[0])
                stageA1b(pre_units[0])
                for i in range(n_ + 2):
                    if i < n_:
                        stageA2(pre_units[i])
                    if i + 1 < n_:
                        stageA1(pre_units[i + 1])
                    if 0 <= i - 1 < n_:
                        stageB(pre_units[i - 1])
                    if i + 1 < n_:
                        stageA1b(pre_units[i + 1])
                    if 0 <= i - 2 < n_:
                        stageC(pre_units[i - 2])
                if carry_pending[0]:
                    carry_finish()
                pi = nextps()
                E("pe", "matmul", ps[pi][:, 0:32], rot, Scar[:, 0, :], start=True, stop=True, r=["cst", "Scar"], w=[("ps", pi)])
                E("dve", "tensor_copy", out=Scar[:, 1, :], in_=ps[pi][:, 0:32], r=[("ps", pi)], w=["Scar"])
                act(Cst_b[:].rearrange("p h d -> p (h d)"), Cst_f[:].rearrange("p h d -> p (h d)"), AF.Copy, ["Cst_f"], ["Cst_b"])
                E("dve", "tensor_copy", out=hal[:, 4:8, :], in_=halb[:], r=["halb"], w=["hal"])
                S.barrier()
            unit_list = own_units

        CZ = sb("CZ2", [128, 32, 9, 16], BF16)
        S.dma("sp", CZ[:].rearrange("p g d c -> p (g d c)"), CZ_d, reads=["CZ_d"], writes=["CZ"])
        xts = [sb(f"xt{i}", [128, 4, DM]) for i in range(2)]
        hT2 = sb("hT2", [128, 8, UT], BF16)
        xsq = sb("xsq", [128, DM])
        XQ = [("xq", i) for i in range(4)]
        hT = sb("hT", [128, 8, UT], BF16)
        uT = sb("uT", [128, 4, UT], BF16); osT = sb("osT", [128, 4, UT], BF16)
        hmT = sb("hmT", [128, 4, UT], BF16); y2T = sb("y2T", [128, 4, UT], BF16)
        rstats = [sb(f"rstat{i}", [128, 8]) for i in range(2)]; bstn = sb("bstn", [128, 2, 6]); bagn = sb("bagn", [128, 2])
        gat = sb("gat", [128, 4, 8]); spl = sb("spl", [128, 4, 4])
        wq = sb("wq", [128, 4, 4]); ga = sb("ga", [128, 4, 4]); gk = sb("gk", [128, 4, 4]); gkL = sb("gkL", [128, 4, 4])
        dec = sb("dec", [128, 4, 4])
        Sms = [sb(f"Sm{i}", [128, 4, 128], BF16) for i in range(2)]
        khats = [sb(f"khat{i}", [128, 4, 128], BF16) for i in range(2)]
        eps_ = [sb(f"ep{i}", [128, 10, 4]) for i in range(2)]
        hmtms = [sb(f"hmtm{i}", [128, 4, 128], BF16) for i in range(2)]
        hsq = sb("hsq", [128, 4, 128])
        W1s = [sb(f"W1s{i}", [128, 256], BF16) for i in range(2)]
        W2ss = [sb(f"W2ss{i}", [128, 256], BF16) for i in range(2)]
        ztp = sb("ztp", [128, 256]); ztq = sb("ztq", [128, 256]); cry = sb("cry", [128, 4, 32])
        wring = [sb(f"wring{i}", [128, DM], BF16) for i in range(3)]
        NWR = 3
        mask01 = sb("mask01", [128, 32, 65], BF16)
        E("pool", "memset", mask01[:], 1.0, w=["mask01"])
        E("pool", "memset", mask01[:, :, 0:1], 0.0, w=["mask01"])
        arena1 = sb("arena1", [128, 8320])
        Zaug = arena1[:, 0:2080].rearrange("p (g n) -> p g n", g=32)
        Zaug2 = arena1[:, 2080:4160].rearrange("p (g n) -> p g n", g=32)
        Ssc = arena1[:, 4160:6240].rearrange("p (g n) -> p g n", g=32)
        Ssc2 = arena1[:, 6240:8320].rearrange("p (g n) -> p g n", g=32)
        Ybm5 = arena1[0:64, 0:2048].bitcast(BF16).rearrange("p (j t g c) -> p j t g c", j=4, t=8, g=8)
        aT = arena1[:, 0:8192].bitcast(BF16).rearrange("p (c t) -> p c t", c=32)
        PWTf = arena1[:, 4160:8256].bitcast(BF16)
        PWT = PWTf.rearrange("p (j e s m) -> p j e s m", j=4, e=2, s=8)
        for k_ in ["Zaug", "Zaug2", "Ssc", "Ssc2", "Ybm", "aT", "PWT"]:
            S.alias[k_] = "arena1"
        arena2 = sb("arena2", [128, 5136])
        a2b = arena2[:].bitcast(BF16)
        xnr2 = [a2b[:, 0:1024], a2b[:, 1024:2048]]
        qT = a2b[:, 2048:4096].rearrange("p (c t) -> p c t", c=4)
        kT = a2b[:, 4096:6144].rearrange("p (c t) -> p c t", c=4)
        pre = arena2[:, 3072:3072 + UT + 3]
        cacc = arena2[:, 3590:3590 + UT]
        vtm = a2b[:, 8204:8204 + 2064].rearrange("p (t h d) -> p t h d", t=4, h=4)
        zT = a2b[:, 0:2048].rearrange("p (c t) -> p c t", c=4)
        sgT = a2b[:, 2048:4096].rearrange("p (c t) -> p c t", c=4)
        Xb = a2b[:, 4096:6144].rearrange("p (g n) -> p g n", g=32)
        Yblk = a2b[:, 6144:8192].rearrange("p (g n) -> p g n", g=32)
        for k_ in ["xn", "qk", "pre", "cacc", "vtm", "zT", "sgT", "Xb", "Yblk"]:
            S.alias[k_] = "arena2"
        wrc = [0]
        pending_tail = []

        def wslot():
            i = wrc[0] % NWR
            wrc[0] += 1
            return i

        def norm_stats(xt, kx, rstat, kr):
            for t in range(4):
                E("dve", "bn_stats", out=bstn[:, 0, :], in_=xt[:, t, 0:512], r=[kx], w=["bstn"])
                E("dve", "bn_stats", out=bstn[:, 1, :], in_=xt[:, t, 512:1024], r=[kx], w=["bstn"])
                E("dve", "bn_aggr", out=bagn[:], in_=bstn[:].rearrange("p a b -> p (a b)"), r=["bstn"], w=["bagn"])
                stt("dve", rstat[:, t:t + 1], bagn[:, 0:1], bagn[:, 0:1], bagn[:, 1:2], ALU.mult, ALU.add, ["bagn"], [kr])
            ts("dve", rstat[:, 4:8], rstat[:, 0:4], EPS, None, ALU.add, ALU.bypass, [kr], [kr])
            act(rstat[:, 4:8], rstat[:, 4:8], AF.Sqrt, [kr], [kr])
            E("dve", "reciprocal", out=rstat[:, 4:8], in_=rstat[:, 4:8], r=[kr], w=[kr])

        def norm_T(xt, kx, rstat, kr, gcol, tag, hdst, kh):
            pbs = []
            for t in range(4):
                ni = t % 2
                act(xnr2[ni], xt[:, t, :], AF.Copy, [kx, kr], ["xn"], scale=rstat[:, 4 + t:5 + t])
                if t % 2 == 0:
                    pbs = [nextps(), nextps()]
                for c in range(8):
                    pi = pbs[c // 4]
                    pb = ps[pi][:].bitcast(BF16)
                    col = ((c % 4) * 2 + (t % 2)) * 128
                    E("pe", "transpose", pb[:, col:col + 128], xnr2[ni][:, c * 128:(c + 1) * 128], ident_b[:],
                      r=["xn", "ident_b"], w=[("ps", pi)])
                if t % 2 == 1:
                    for half in range(2):
                        pi = pbs[half]
                        pb = ps[pi][:].bitcast(BF16)
                        for cc in range(4):
                            c = half * 4 + cc
                            dst = hdst[:, c, (t - 1) * 128:(t + 1) * 128]
                            src = pb[:, cc * 256:(cc + 1) * 256]
                            if half == 0:
                                act(dst, src, AF.Copy, [("ps", pi), tag], [(kh, c)], scale=gcol[:, c:c + 1])
                            else:
                                ts("dve", dst, src, gcol[:, c:c + 1], None, ALU.mult, ALU.bypass, [("ps", pi), tag], [(kh, c)])

        def proj_fm(col0, nchunks, evac):
            for c in range(nchunks):
                bi = wslot()
                wb = wring[bi][:].rearrange("p (k m) -> p k m", k=8)
                S.dma("sp", wb, win_s[:, :, col0 + c * 128:col0 + (c + 1) * 128], reads=["win_s"], writes=[("wr", bi)])
                pi = nextps()
                for kc in range(8):
                    E("pe", "matmul", ps[pi][:], wb[:, kc, :], hT[:, kc, :], start=(kc == 0), stop=(kc == 7),
                      r=[("wr", bi), ("hT", kc)], w=[("ps", pi)])
                evac(c, pi)

        def conv_chunk(c, pi, dstT):
            E("pool", "tensor_copy", out=pre[:, 0:3], in_=hal[:, c, :], r=["hal"], w=["pre"])
            act(pre[:, 3:UT + 3], ps[pi][:], AF.Copy, [("ps", pi)], ["pre"])
            E("pool", "tensor_copy", out=hal[:, c, :], in_=pre[:, UT:UT + 3], r=["pre"], w=["hal"])
            ts("dve", cacc, pre[:, 0:UT], cw[:, c, 0:1], cb[:, c:c + 1], ALU.mult, ALU.add, ["pre", "cw", "cb"], ["cacc"])
            for jj in range(1, 4):
                stt("dve", cacc, pre[:, jj:jj + UT], cw[:, c, jj:jj + 1], cacc, ALU.mult, ALU.add,
                    ["pre", "cw", "cacc"], ["cacc"])
            act(dstT, cacc, AF.Silu, ["cacc"], [("qk", c)])

        Er5 = Er[:].rearrange("p (j q e) n -> p j q e n", j=4, q=4)
        Ei5 = Ei[:].rearrange("p (j q e) n -> p j q e n", j=4, q=4)
        Za5 = Zaug.rearrange("p (j q e) n -> p j q e n", j=4, q=4)
        Zb5 = Zaug2.rearrange("p (j q e) n -> p j q e n", j=4, q=4)

        def partA(u):
            seg = u // 4
            tok0 = u * UT
            b = u % 2
            xt = xts[b]; kx = ("xt", b); rstat = rstats[b]; kr = ("rstat", b)
            for t in range(4):
                S.dma("sp", xt[:, t, :], xall[tok0 + t * 128: tok0 + (t + 1) * 128, :], writes=[kx])
            norm_stats(xt, kx, rstat, kr)
            norm_T(xt, kx, rstat, kr, g1, "g1", hT, "hT")
            proj_fm(0, 4, lambda c, pi: conv_chunk(c, pi, qT[:, c, :]))
            proj_fm(512, 4, lambda c, pi: conv_chunk(4 + c, pi, kT[:, c, :]))
            proj_fm(2056, 4, lambda c, pi: act(uT[:, c, :], ps[pi][:], AF.Copy, [("ps", pi)], ["uT"]))
            proj_fm(1536, 4, lambda c, pi: act(osT[:, c, :], ps[pi][:], AF.Sigmoid, [("ps", pi)], ["osT"]))
            E("pool", "memset", vtm[:, :, :, 128:129], 1.0, w=["vtm"])
            pg = nextps()
            for t in range(4):
                pi = nextps()
                for kc in range(8):
                    E("pe", "matmul", ps[pi][:], hT[:, kc, t * 128:(t + 1) * 128], wvg[:, kc, 0:512], start=(kc == 0), stop=(kc == 7),
                      r=["wvg", ("hT", kc)], w=[("ps", pi)])
                act(vtm[:, t, :, 0:128], ps[pi][:].rearrange("p (h d) -> p h d", h=4), AF.Copy, [("ps", pi)], ["vtm"])
                for kc in range(8):
                    E("pe", "matmul", ps[pg][:, t * 8:(t + 1) * 8], hT[:, kc, t * 128:(t + 1) * 128], wvg[:, kc, 512:520],
                      start=(kc == 0), stop=(kc == 7), r=["wvg", ("hT", kc)], w=[("ps", pg)])
            tt("dve", gat[:], ps[pg][:, 0:32].rearrange("p (t g) -> p t g", t=4), gbias[:].unsqueeze(1).to_broadcast([128, 4, 8]),
               ALU.add, [("ps", pg), "gbias"], ["gat"])
            act(spl[:], gat[:, :, 4:8], AF.Exp, ["gat"], ["spl"], scale=-1.0)
            act(spl[:], spl[:], AF.Ln, ["spl"], ["spl"], bias=1.0)
            pq = nextps()
            sp2 = spl[:].rearrange("p t h -> p (t h)")
            E("pe", "matmul", ps[pq][:, 0:16], tri[:], sp2, start=True, stop=True, r=["tri", "spl"], w=[("ps", pq)])
            E("pe", "matmul", ps[pq][:, 16:32], ones_f[:], sp2, start=True, stop=True, r=["ones_f", "spl"], w=[("ps", pq)])
            cums = ps[pq][:, 0:16].rearrange("p (t h) -> p t h", t=4)
            tot = ps[pq][:, 16:32].rearrange("p (t h) -> p t h", t=4)
            G = [("ps", pq), "gat"]
            act(wq[:], cums, AF.Exp, G, ["wq"], scale=-1.0, bias=-0.5 * math.log(128.0))
            tt("dve", ga[:], gat[:, :, 0:4], cums, ALU.add, G, ["ga"])
            act(gk[:], ga[:], AF.Exp, ["ga"], ["gk"])
            tt("dve", ga[:], ga[:], tot, ALU.subtract, G + ["ga"], ["ga"])
            act(gkL[:], ga[:], AF.Exp, ["ga"], ["gkL"])
            ts("dve", gkL[:], gkL[:], segm[:, seg:seg + 1], None, ALU.mult, ALU.bypass, ["gkL", "segm"], ["gkL"])
            act(dec[:], tot, AF.Exp, G, ["dec"], scale=-1.0)

        def partB(u):
            b = u % 2
            xt = xts[b]; kx = ("xt", b)


            s5bk = {}

            def s5_mm(bb):
                bk = [nextps(pin=True) for _ in range(4)]
                s5bk[bb] = bk
                for jj in range(2):
                    j = 2 * bb + jj
                    for e_ in range(2):
                        for sg in range(8):
                            for pp in range(4):
                                pi = bk[pp]
                                pv = ps[pi][:].rearrange("p (w j e n) -> p w j e n", w=2, j=2, e=2)
                                E("pe", "matmul", pv[:, 0, jj, e_, :], PWB[32 * pp:32 * pp + 32, j, e_, sg, :],
                                  uT[32 * pp:32 * pp + 32, j, sg:UT:8], start=(sg == 0), stop=(sg == 7),
                                  tile_position=(32 * pp, 0), r=["PWB", "uT"], w=[("ps", pi)])

            def s5_post(bb):
                bk = s5bk[bb]
                for pp in range(4):
                    pi = bk[pp]
                    ri = pp % 2
                    W1 = W1s[ri]
                    W2s = W2ss[ri]
                    k1, k2 = ("W1", ri), ("W2s", ri)
                    act(W1[:], ps[pi][:, 0:256], AF.Copy, [("ps", pi)], [k1])
                    E("pe", "matmul", ps[pi][:, 256:512], rot_b[:], W1[:], start=True, stop=True, r=["rot_b", k1], w=[("ps", pi)])
                    act(W2s[:], ps[pi][:, 256:512], AF.Copy, [("ps", pi)], [k2])
                    w1v = W1[:].rearrange("p (j e n) -> p j e n", j=2, e=2)
                    w2v = W2s[:].rearrange("p (j e n) -> p j e n", j=2, e=2)
                    zv = ztp[:].rearrange("p (j e n) -> p j e n", j=2, e=2)
                    zq = ztq[:].rearrange("p (j e n) -> p j e n", j=2, e=2)
                    js = slice(2 * bb, 2 * bb + 2)
                    zo = Za5[:, js, pp, :, 1:65]
                    zo2 = Zb5[:, js, pp, :, 1:65]
                    tt("dve", zo, Er5[:, js, pp, :, :], w1v, ALU.mult, ["Etab", k1], ["Zaug"])
                    tt("dve", zv, Ei5[:, js, pp, :, :], w2v, ALU.mult, ["Etab", k2], ["ztp"])
                    tt("dve", zo, zo, zv, ALU.add, ["Zaug", "ztp"], ["Zaug"])
                    tt("pool", zo2, Er5[:, js, pp, :, :], w2v, ALU.mult, ["Etab", k2], ["Zaug2"])
                    tt("pool", zq, Ei5[:, js, pp, :, :], w1v, ALU.mult, ["Etab", k1], ["ztq"])
                    tt("pool", zo2, zo2, zq, ALU.subtract, ["Zaug2", "ztq"], ["Zaug2"])
                unpin(bk)

            Smfs = [xsq[:, 0:512].rearrange("p (h d) -> p h d", h=4), xsq[:, 512:1024].rearrange("p (h d) -> p h d", h=4)]
            KSm = [[XQ[0], XQ[1]], [XQ[2], XQ[3]]]
            st = {}

            def P1(t):
                par = t % 2
                tc_ = slice(t * 128, (t + 1) * 128)
                pi = nextps()
                for h in range(4):
                    E("pe", "matmul", ps[pi][:, h * 128:(h + 1) * 128], kT[:, h, tc_], qT[:, h, tc_], start=True, stop=True,
                      r=[("qk", 4 + h), ("qk", h)], w=[("ps", pi)])
                pk = nextps()
                pkb = ps[pk][:].bitcast(BF16)
                for h in range(4):
                    E("pe", "transpose", pkb[:, h * 128:(h + 1) * 128], kT[:, h, tc_], ident_b[:], r=[("qk", 4 + h), "ident_b"], w=[("ps", pk)])
                tt("dve", Smfs[par], ps[pi][:].rearrange("p (h d) -> p h d", h=4), gk[:, t, :].unsqueeze(2).to_broadcast([128, 4, 128]),
                   ALU.mult, [("ps", pi), "gk"], KSm[par])
                tt("pool", Sms[par][:], Smfs[par], trimask[:].unsqueeze(1).to_broadcast([128, 4, 128]), ALU.mult, KSm[par] + ["trimask"], [("Sm", par)])
                tt("dve", khats[par][:], pkb[:, 0:512].rearrange("p (h d) -> p h d", h=4), gkL[:, t, :].unsqueeze(2).to_broadcast([128, 4, 128]),
                   ALU.mult, [("ps", pk), "gkL"], [("khat", par)])

            def P2(t):
                par = t % 2
                tc_ = slice(t * 128, (t + 1) * 128)
                pn = [nextps(pin=True), nextps(pin=True)]
                st[t] = pn
                for h in range(4):
                    o_ = ps[pn[h // 2]][:, (h % 2) * 129:(h % 2) * 129 + 129]
                    E("pe", "matmul", o_, Sms[par][:, h, :], vtm[:, t, h, :], start=True, stop=False, r=[("Sm", par), "vtm"], w=[("ps", pn[h // 2])])
                    E("pe", "matmul", o_, qT[:, h, tc_], Cst_b[:, h, :], start=False, stop=True, r=[("qk", h), "Cst_b"], w=[("ps", pn[h // 2])])
                pc = [nextps(), nextps()]
                for h in range(4):
                    E("pe", "matmul", ps[pc[h // 2]][:, (h % 2) * 129:(h % 2) * 129 + 129], khats[par][:, h, :], vtm[:, t, h, :],
                      start=True, stop=True, r=[("khat", par), "vtm"], w=[("ps", pc[h // 2])])
                for hp in range(2):
                    cs = Cst_f[:, 2 * hp:2 * hp + 2, :]
                    tt("dve", cs, cs, dec[:, t, 2 * hp:2 * hp + 2].unsqueeze(2).to_broadcast([128, 2, 129]), ALU.mult, ["Cst_f", "dec"], ["Cst_f"])
                    tt("dve", cs, cs, ps[pc[hp]][:, 0:258].rearrange("p (h d) -> p h d", h=2), ALU.add, ["Cst_f", ("ps", pc[hp])], ["Cst_f"])
                act(Cst_b[:].rearrange("p h d -> p (h d)"), Cst_f[:].rearrange("p h d -> p (h d)"), AF.Copy, ["Cst_f"], ["Cst_b"])

            def P3(t):
                par = t % 2
                pn = st[t]
                e_ = eps_[par]
                ke = ("ep", par)
                PN = [("ps", pn[0]), ("ps", pn[1])]
                for hp in range(2):
                    act(e_[:, 0, 2 * hp:2 * hp + 2], ps[pn[hp]][:, 128:258:129], AF.Abs, [PN[hp]], [ke])
                tt("dve", e_[:, 1, :], e_[:, 0, :], wq[:, t, :], ALU.mult, [ke, "wq"], [ke])
                ts("dve", e_[:, 1, :], e_[:, 1, :], 1.0, None, ALU.max, ALU.bypass, [ke], [ke])
                E("dve", "reciprocal", out=e_[:, 2, :], in_=e_[:, 1, :], r=[ke], w=[ke])
                tt("dve", e_[:, 3, :], e_[:, 2, :], wq[:, t, :], ALU.mult, [ke, "wq"], [ke])
                hv4 = Smfs[par]
                for hp in range(2):
                    tt("dve", hv4[:, 2 * hp:2 * hp + 2, :], ps[pn[hp]][:, 0:258].rearrange("p (h d) -> p h d", h=2)[:, :, 0:128],
                       e_[:, 3, 2 * hp:2 * hp + 2].unsqueeze(2).to_broadcast([128, 2, 128]), ALU.mult, [PN[hp], ke, ("Sm", par)], KSm[par])
                E("dve", "reduce_sum", out=e_[:, 4, :], in_=hv4, axis=AX.X, r=KSm[par], w=[ke])
                tt("pool", hsq[:], hv4, hv4, ALU.mult, KSm[par], ["hsq"])
                E("dve", "reduce_sum", out=e_[:, 5, :], in_=hsq[:], axis=AX.X, r=["hsq"], w=[ke])
                ts("dve", e_[:, 6, :], e_[:, 4, :], 1.0 / 128.0, None, ALU.mult, ALU.bypass, [ke], [ke])
                tt("dve", e_[:, 7, :], e_[:, 6, :], e_[:, 6, :], ALU.mult, [ke], [ke])
                stt("dve", e_[:, 8, :], e_[:, 5, :], 1.0 / 128.0, e_[:, 7, :], ALU.mult, ALU.subtract, [ke], [ke])
                ts("dve", e_[:, 8, :], e_[:, 8, :], EPS, None, ALU.add, ALU.bypass, [ke], [ke])
                act(e_[:, 8, :], e_[:, 8, :], AF.Sqrt, [ke], [ke])
                E("dve", "reciprocal", out=e_[:, 9, :], in_=e_[:, 8, :], r=[ke], w=[ke])
                tt("dve", hv4, hv4, e_[:, 6, :].unsqueeze(2).to_broadcast([128, 4, 128]), ALU.subtract, KSm[par] + [ke], KSm[par])
                tt("dve", hmtms[par][:], hv4, e_[:, 9, :].unsqueeze(2).to_broadcast([128, 4, 128]), ALU.mult, KSm[par] + [ke], [("hmtm", par)])
                unpin(pn)

            def P4(t):
                par = t % 2
                tc_ = slice(t * 128, (t + 1) * 128)
                pt_ = nextps()
                ptb = ps[pt_][:].bitcast(BF16)
                for h in range(4):
                    E("pe", "transpose", ptb[:, h * 128:(h + 1) * 128], hmtms[par][:, h, :], ident_b[:], r=[("hmtm", par), "ident_b"], w=[("ps", pt_)])
                tt("dve", hmT[:, :, tc_], ptb[:, 0:512].rearrange("p (h d) -> p h d", h=4), mnwc[:].unsqueeze(2).to_broadcast([128, 4, 128]),
                   ALU.mult, [("ps", pt_), "mnwc"], ["hmT"])
                tt("pool", hmT[:, :, tc_], hmT[:, :, tc_], osT[:, :, tc_], ALU.mult, ["hmT", "osT"], ["hmT"])

            P1(0)
            s5_mm(0)
            P2(0)
            P1(1)
            s5_post(0)
            P3(0)
            s5_mm(1)
            P2(1)
            P1(2)
            s5_post(1)
            P3(1)
            P4(0)
            E("dve", "tensor_copy", out=Zaug[:, :, 0], in_=Scar[:, 0, :], r=["Scar"], w=["Zaug"])
            E("pool", "tensor_copy", out=Zaug2[:, :, 0], in_=Scar[:, 1, :], r=["Scar"], w=["Zaug2"])
            m01 = mask01[:].rearrange("p g n -> p (g n)")
            E("dve", "tensor_tensor_scan", out=Ssc.rearrange("p g n -> p (g n)"), data0=m01, data1=Zaug.rearrange("p g n -> p (g n)"),
              initial=0.0, op0=ALU.mult, op1=ALU.add, r=["Zaug", "mask01"], w=["Ssc"])
            E("dve", "tensor_tensor_scan", out=Ssc2.rearrange("p g n -> p (g n)"), data0=m01, data1=Zaug2.rearrange("p g n -> p (g n)"),
              initial=0.0, op0=ALU.mult, op1=ALU.add, r=["Zaug2", "mask01"], w=["Ssc2"])
            S.dma("sp", Zaug[:, :, 0:64], Fr_d.rearrange("p (g n) -> p g n", g=32), reads=["Fr_d", "Zaug"], writes=["Zaug"])
            S.dma("sp", Zaug2[:, :, 0:64], Fi_d.rearrange("p (g n) -> p g n", g=32), reads=["Fi_d", "Zaug2"], writes=["Zaug2"])
            tt("dve", Zaug[:, :, 0:64], Zaug[:, :, 0:64], Ssc[:, :, 0:64], ALU.mult, ["Zaug", "Ssc"], ["Zaug"])
            tt("pool", Zaug2[:, :, 0:64], Zaug2[:, :, 0:64], Ssc2[:, :, 0:64], ALU.mult, ["Zaug2", "Ssc2"], ["Zaug2"])
            l5r = L5r[:, 0, :]; l5i = L5i[:, 0, :]
            tt("dve", cry[:, 0, :], l5r, Ssc[:, :, 64], ALU.mult, ["L5", "Ssc"], ["cry"])
            tt("dve", cry[:, 1, :], l5i, Ssc2[:, :, 64], ALU.mult, ["L5", "Ssc2"], ["cry"])
            tt("dve", cry[:, 2, :], l5r, Ssc2[:, :, 64], ALU.mult, ["L5", "Ssc2"], ["cry"])
            tt("dve", cry[:, 3, :], l5i, Ssc[:, :, 64], ALU.mult, ["L5", "Ssc"], ["cry"])
            tt("dve", Scar[:, 0, :], cry[:, 0, :], cry[:, 1, :], ALU.add, ["cry"], ["Scar"])
            tt("dve", Scar[:, 1, :], cry[:, 2, :], cry[:, 3, :], ALU.subtract, ["cry"], ["Scar"])
            S.dma("sp", PWTf, PWT_d, reads=["PWT_d", "Ssc", "Ssc2"], writes=["PWT"])
            P2(2)
            P1(3)
            P3(2)
            P4(1)
            P2(3)
            tt("dve", Xb, Zaug[:, :, 0:64], Zaug2[:, :, 0:64], ALU.add, ["Zaug", "Zaug2"], ["Xb"])
            for j in range(4):
                bk = [nextps() for _ in range(4)]
                for e_ in range(2):
                    for sg in range(8):
                        for pp in range(4):
                            pi = bk[pp]
                            pv = ps[pi][:, 0:128].rearrange("p (g n) -> p g n", g=2)
                            E("pe", "matmul", pv[:, e_, :], PWT[32 * pp:32 * pp + 32, j, e_, sg, :],
                              uT[32 * pp:32 * pp + 32, j, sg:UT:8], start=(sg == 0), stop=False,
                              tile_position=(32 * pp, 0), r=["PWT", "uT"], w=[("ps", pi)])
                    for pp in range(4):
                        pi = bk[pp]
                        pv = ps[pi][:, 0:128].rearrange("p (g n) -> p g n", g=2)
                        g = 8 * j + 2 * pp + e_
                        E("pe", "matmul", pv[:, e_, :], CZ[:, g, 1:9, :].rearrange("p d c -> p (d c)"), Xb[:, g, :],
                          start=False, stop=True, r=["CZ", "Xb"], w=[("ps", pi)])
                for pp in range(4):
                    pi = bk[pp]
                    g0 = 8 * j + 2 * pp
                    act(Yblk[:, g0:g0 + 2, :], ps[pi][:, 0:128].rearrange("p (g n) -> p g n", g=2), AF.Copy, [("ps", pi)], ["Yblk"])
            P3(3)
            P4(2)
            P4(3)
            if pending_tail:
                pending_tail.pop(0)()
            for q4 in range(8):
                pi = nextps()
                pb = ps[pi][:].bitcast(BF16)
                for gi in range(4):
                    g = q4 * 4 + gi
                    E("pe", "transpose", pb[0:64, gi * 128:(gi + 1) * 128], Yblk[:, g, :], ident_b[:], r=["Yblk", "ident_b"], w=[("ps", pi)])
                E("dve" if q4 % 2 == 0 else "act", "tensor_copy" if q4 % 2 == 0 else "copy",
                  out=Ybm5[:, q4 // 2, :, 4 * (q4 % 2):4 * (q4 % 2) + 4, :],
                  in_=pb[0:64, 0:512].rearrange("p (g t c) -> p t g c", g=4, t=8), r=[("ps", pi)], w=["Ybm"])
            for j in range(4):
                pi = nextps()
                pb = ps[pi][:].bitcast(BF16)
                for tau in range(8):
                    E("pe", "transpose", pb[:, tau * 64:(tau + 1) * 64], Ybm5[:, j, tau, :, :].rearrange("p g c -> p (g c)"), ident_b[0:64, 0:64],
                      r=["Ybm", "ident_b"], w=[("ps", pi)])
                act(zT[:, j, :].rearrange("p (n t) -> p t n", t=8), pb[:, 0:512].rearrange("p (t n) -> p t n", t=8), AF.Gelu,
                    [("ps", pi)], ["zT"])
            for jo in range(4):
                bi = wslot()
                wb = wring[bi][:, 0:512].rearrange("p (k m) -> p k m", k=4)
                S.dma("pool", wb, d_gluw[:, :, jo * 128:(jo + 1) * 128], writes=[("wr", bi)])
                pi = nextps()
                for ji in range(4):
                    E("pe", "matmul", ps[pi][:], wb[:, ji, :], zT[:, ji, :], start=(ji == 0), stop=(ji == 3),
                      r=[("wr", bi), "zT"], w=[("ps", pi)])
                act(sgT[:, jo, :], ps[pi][:], AF.Sigmoid, [("ps", pi), "glub"], ["sgT"], bias=glub[:, jo:jo + 1])
            tt("dve", y2T[:], zT, sgT, ALU.mult, ["zT", "sgT"], ["y2T"])
            pis = [nextps() for _ in range(8)]
            for kc in range(8):
                bi = wslot()
                wb = wring[bi]
                S.dma("pool", wb[:], d_wout[:, kc, :], writes=[("wr", bi)])
                for t in range(4):
                    tc_ = slice(t * 128, (t + 1) * 128)
                    lh = hmT[:, kc, tc_] if kc < 4 else y2T[:, kc - 4, tc_]
                    for hf in range(2):
                        pi = pis[t * 2 + hf]
                        E("pe", "matmul", ps[pi][:], lh, wb[:, hf * 512:(hf + 1) * 512], start=(kc == 0), stop=(kc == 7),
                          r=["hmT", "y2T", ("wr", bi)], w=[("ps", pi)])
            for t in range(4):
                for hf in range(2):
                    pi = pis[t * 2 + hf]
                    tt("dve", xt[:, t, hf * 512:(hf + 1) * 512], xt[:, t, hf * 512:(hf + 1) * 512], ps[pi][:], ALU.add,
                       [kx, ("ps", pi)], [kx])

        def partC(u):
            b = u % 2
            xt = xts[b]; kx = ("xt", b); rstat = rstats[b]; kr = ("rstat", b)
            norm_stats(xt, kx, rstat, kr)
            norm_T(xt, kx, rstat, kr, g2, "g2", hT2, "hT2")
            for fc in range(32):
                bi = wslot()
                wb = wring[bi][:].rearrange("p (k m) -> p k m", k=8)
                S.dma("sp", wb, wff1_s[fc], reads=[("wff1_s", fc)], writes=[("wr", bi)])
                pi = nextps()
                for kc in range(8):
                    E("pe", "matmul", ps[pi][:], wb[:, kc, :], hT2[:, kc, :], start=(kc == 0), stop=(kc == 7),
                      r=[("wr", bi), ("hT2", kc)], w=[("ps", pi)])
                ri = fc % 2
                rl = xsq[:, ri * 512:(ri + 1) * 512]
                act(rl, ps[pi][:], AF.Relu, [("ps", pi)], [XQ[2 * ri], XQ[2 * ri + 1]])
                tt("pool", aT[:, fc, :], rl, rl, ALU.mult, [XQ[2 * ri], XQ[2 * ri + 1]], ["aT"])
            pis = [nextps() for _ in range(8)]
            for fc in range(32):
                bi = wslot()
                wb = wring[bi]
                S.dma("sp", wb[:], wff2_s[fc], reads=[("wff2_s", fc)], writes=[("wr", bi)])
                for t in range(4):
                    for hf in range(2):
                        pi = pis[t * 2 + hf]
                        E("pe", "matmul", ps[pi][:], aT[:, fc, t * 128:(t + 1) * 128], wb[:, hf * 512:(hf + 1) * 512],
                          start=(fc == 0), stop=(fc == 31), r=["aT", ("wr", bi)], w=[("ps", pi)])
            for t in range(4):
                for hf in range(2):
                    pi = pis[t * 2 + hf]
                    tt("dve", xt[:, t, hf * 512:(hf + 1) * 512], xt[:, t, hf * 512:(hf + 1) * 512], ps[pi][:], ALU.add,
                       [kx, ("ps", pi)], [kx])
            last = (u == unit_list[-1])

            def tail():
                norm_stats(xt, kx, rstat, kr)
                for t in range(4):
                    r0 = (u - FIRST_OWN) * UT + t * 128
                    if last:
                        fb = arena1[:, t * DM:(t + 1) * DM]
                        kf = ("fin", t)
                        act(fb, xt[:, t, :], AF.Copy, [kx, kr], [kf], scale=rstat[:, 4 + t:5 + t])
                        tt("dve", fb, fb, g3[:], ALU.mult, [kf, "g3"], [kf])
                        S.dma("sp", out[r0:r0 + 128, :], fb, reads=[kf], is_output=True)
                    else:
                        act(xsq[:], xt[:, t, :], AF.Copy, [kx, kr], XQ, scale=rstat[:, 4 + t:5 + t])
                        tt("pool", xsq[:], xsq[:], g3[:], ALU.mult, XQ + ["g3"], XQ)
                        S.dma("sp", out[r0:r0 + 128, :], xsq[:], reads=XQ, is_output=True)
            pending_tail.append(tail)


        n_own = len(unit_list)
        if n_own:
            partA(unit_list[0])
        for i in range(n_own):
            partB(unit_list[i])
            if i + 1 < n_own:
                partA(unit_list[i + 1])
            partC(unit_list[i])
        while pending_tail:
            pending_tail.pop(0)()
        if dbg_sel is not None:
            S.dma("sp", dbg2[:, 32760:32768], rstats[0][:], reads=[("rstat", 0)], is_output=True)

        with nc.Block() as block:
            S.finish(block)
    return nc


def _host_layout(inp, core):
    f = np.float32
    x = np.asarray(inp["x"], f)[0]
    d = {}
    xall = np.zeros((NSEG * TOK, DM), f)
    n_real = (core + 1) * TOK
    xall[NSEG * TOK - n_real:] = x[:n_real]
    d["xall"] = xall
    segm = np.zeros((128, 8), f)
    segm[:, NSEG - (core + 1):] = 1.0
    d["segm"] = segm
    return d


def _host_shared(inp):
    f = np.float32
    g = lambda k: np.asarray(inp[k], f)
    d = {}
    d["w_in"] = np.ascontiguousarray(g("w_in")[0].reshape(8, 128, INC).transpose(1, 0, 2))
    d["w_out"] = np.ascontiguousarray(g("w_out")[0].reshape(8, 128, DM).transpose(1, 0, 2))
    d["glu_w"] = np.ascontiguousarray(g("glu_w")[0].reshape(4, 128, 512).transpose(1, 0, 2))
    d["w_ff1"] = np.ascontiguousarray(g("w_ff1")[0].reshape(8, 128, 32, 128).transpose(2, 1, 0, 3))
    d["w_ff2"] = np.ascontiguousarray(g("w_ff2")[0].reshape(32, 128, DM))
    d["cw"] = np.ascontiguousarray(g("conv_w")[0].reshape(4, 8, 128).transpose(2, 1, 0))
    d["cb"] = np.ascontiguousarray(g("conv_b")[0].reshape(8, 128).T)
    d["gbias"] = np.ascontiguousarray(np.broadcast_to(np.concatenate([g("i_bias")[0], g("f_bias")[0]])[None, :], (128, 8)))
    d["mnwc"] = np.ascontiguousarray(g("mlstm_norm_w")[0].reshape(4, 128).T)
    d["g1"] = np.ascontiguousarray(g("mix_norm_w")[0].reshape(8, 128).T)
    d["g2"] = np.ascontiguousarray(g("mlp_norm_w")[0].reshape(8, 128).T)
    d["g3"] = np.ascontiguousarray(np.broadcast_to(g("final_norm_w")[None, :], (128, DM)))
    d["glub"] = np.ascontiguousarray(g("glu_b")[0].reshape(4, 128).T)
    lr, li, ld = g("ssm_lam_re")[0], g("ssm_lam_im")[0], g("ssm_log_dt")[0]
    d["lr2"] = np.ascontiguousarray(np.tile(lr.T, (2, 1)))
    d["li2"] = np.ascontiguousarray(np.tile(li.T, (2, 1)))
    d["ld2"] = np.ascontiguousarray(np.broadcast_to(ld[None, :], (128, 32)))

    def tl1(a):
        a4 = a.reshape(4, 8, 64)
        o = np.broadcast_to(a4.transpose(1, 0, 2)[:, None, :, :], (8, 16, 4, 64))
        return np.ascontiguousarray(o.reshape(128, 256))
    d["lr1"] = tl1(lr); d["li1"] = tl1(li)
    d["ld1"] = tl1(np.broadcast_to(ld[:, None], (32, 64)))
    br, bi = g("ssm_b_re")[0], g("ssm_b_im")[0]
    cr, ci = g("ssm_c_re")[0], g("ssm_c_im")[0]
    bpr = br.transpose(1, 0, 2).reshape(64, 512); bpi = bi.transpose(1, 0, 2).reshape(64, 512)
    d["AB"] = np.ascontiguousarray(np.concatenate([bpr, bpi], 0))
    d["ABs"] = np.ascontiguousarray(np.concatenate([bpi, bpr], 0))
    cpr = cr.transpose(2, 0, 1).reshape(64, 512); cpi = ci.transpose(2, 0, 1).reshape(64, 512)
    d["AC"] = np.ascontiguousarray(np.concatenate([cpr, cpi], 0))
    d["ACs"] = np.ascontiguousarray(np.concatenate([cpi, cpr], 0))

    def tl1b(b):
        b4 = b.reshape(4, 8, 64, 16)
        return np.ascontiguousarray(b4.transpose(1, 3, 0, 2).reshape(128, 256))
    d["bT1r"] = tl1b(br); d["bT1i"] = tl1b(bi)
    D = g("ssm_d")[0].reshape(4, 8, 16)
    dpad = np.zeros((8, 16, 4, 16), f)
    for c in range(16):
        dpad[:, c, :, c] = D[:, :, c].T
    d["dpad"] = np.ascontiguousarray(dpad.reshape(128, 4, 16))
    cst = np.zeros((128, 384), f)
    cst[:, 0:9] = np.arange(9)
    n = np.arange(64)
    cst[:, 16:80] = -8.0 * (n - 32)
    cst[:, 80:144] = 8.0 * (n - 32)
    cst[:, 144:152] = -(np.arange(8) + 1.0)
    cst[:64, 152] = -1.0; cst[64:, 152] = 1.0
    cst[:64, 153] = 1.0; cst[64:, 153] = -1.0
    q = np.arange(128)
    cst[:, 154] = ((q // 16) % 2 == 0); cst[:, 155] = ((q // 16) % 2 == 1)
    for g8 in range(8):
        cst[:, 156 + g8] = (q // 16 == g8)
    cst[:, 164] = 512.0
    cst[:, 165] = 1.0
    rot = np.zeros((128, 128), f)
    for m in range(64):
        rot[m + 64, m] = -1.0
        rot[m, m + 64] = 1.0
    cst[:, 256:384] = rot
    d["cst"] = cst
    d["ident"] = np.eye(128, dtype=f)
    d["tri"] = np.triu(np.ones((128, 128), f))
    return d


_NC_CACHE = {}


def kernel(**inputs):
    if "nc" not in _NC_CACHE:
        _NC_CACHE["nc"] = build_nc()
    nc = _NC_CACHE["nc"]
    shared = _host_shared(inputs)
    in_maps = []
    for c in range(NCORES):
        m = dict(shared)
        m.update(_host_layout(inputs, c))
        in_maps.append(m)
    res = run_bass_kernel_spmd(nc, in_maps, core_ids=list(range(NCORES)))
    outs = [np.asarray(r["out"], np.float32) for r in res.results]
    return np.concatenate(outs, axis=0)[None, :, :]
```
